# Optimizing a Trainium2 kernel written in Bass

```python
import math
import jax, jax.numpy as jnp
from jax import lax
import numpy as np

D_MODEL = 1024
BATCH = 8
SEQ = 2048
DEPTH = 1

DILATED_PATTERNS = ((128, 1), (512, 4), (2048, 16))
N_GROUPS = len(DILATED_PATTERNS)
HEADS_PER_GROUP = 4
N_HEADS_A = N_GROUPS * HEADS_PER_GROUP
HEAD_DIM = 128
QKV_WIDTH = N_HEADS_A * HEAD_DIM
ATTN_OUT = HEADS_PER_GROUP * HEAD_DIM
CONV_DIM = D_MODEL
CONV_WIDTH = 31
D_FF = 2816
FFN_CONV_WIDTH = 3
N_BUCKETS = 32
MAX_DISTANCE = 2048
IN_WIDTH = 3 * QKV_WIDTH + 2 * CONV_DIM + 2 * D_MODEL
RMS_EPS = 1e-6
LN_EPS = 1e-5
NEG_INF = -1e30

kernel_name = "hybrid_dilated_attn_conformer_convffn"


def rms_norm(x, g):
    xf = x.astype(jnp.float32)
    y = xf * lax.rsqrt(jnp.mean(xf * xf, axis=-1, keepdims=True) + RMS_EPS)
    return (y * g.astype(jnp.float32)).astype(x.dtype)


def layer_norm(x, g, b):
    xf = x.astype(jnp.float32)
    mu = jnp.mean(xf, axis=-1, keepdims=True)
    xc = xf - mu
    var = jnp.mean(xc * xc, axis=-1, keepdims=True)
    y = xc * lax.rsqrt(var + LN_EPS) * g.astype(jnp.float32) + b.astype(jnp.float32)
    return y.astype(x.dtype)


def causal_depthwise_conv(x, w, b):
    K, C = w.shape
    y = lax.conv_general_dilated(
        x, w[:, None, :].astype(x.dtype), window_strides=(1,), padding=[(K - 1, 0)],
        dimension_numbers=("NWC", "WIO", "NWC"), feature_group_count=C)
    return y + b.astype(x.dtype)


def t5_bucket(dist):
    max_exact = N_BUCKETS // 2
    is_small = dist < max_exact
    d = jnp.maximum(dist, 1).astype(jnp.float32)
    large = max_exact + (jnp.log(d / max_exact) / math.log(MAX_DISTANCE / max_exact)
                         * (N_BUCKETS - max_exact)).astype(jnp.int32)
    large = jnp.minimum(large, N_BUCKETS - 1)
    return jnp.where(is_small, dist, large)


def dilated_group_attention(q, k, v, bias_g, window, dilation):
    B, S, H, Dh = q.shape
    r = dilation
    span = window // dilation
    L = S // r
    nb = -(-L // span)
    Lp = nb * span

    def split(t):
        t = t.reshape(B, L, r, H, Dh).transpose(0, 2, 1, 3, 4)
        t = jnp.pad(t, ((0, 0), (0, 0), (0, Lp - L), (0, 0), (0, 0)))
        return t.reshape(B, r, nb, span, H, Dh)

    def band(t):
        prev = jnp.pad(t, ((0, 0), (0, 0), (1, 0), (0, 0), (0, 0), (0, 0)))[:, :, :-1]
        return jnp.concatenate([prev, t], axis=3)

    qb = split(q)
    kk = band(split(k))
    vv = band(split(v))

    s = jnp.einsum("brnqhd,brnkhd->brnhqk", qb, kk).astype(jnp.float32) * (Dh ** -0.5)

    qi = jnp.arange(span)[:, None]
    ki = jnp.arange(2 * span)[None, :]
    rel = qi + span - ki
    blk = jnp.arange(nb)[:, None, None]
    valid = (rel >= 0) & (rel <= span) & (blk * span + ki - span >= 0)
    bucket = t5_bucket(jnp.maximum(rel, 0) * r)
    bias = bias_g[bucket].astype(jnp.float32).transpose(2, 0, 1)

    s = jnp.where(valid[None, None, :, None], s + bias, NEG_INF)
    m = jnp.max(s, axis=-1)
    p = jnp.exp(s - m[..., None])
    l = jnp.sum(p, axis=-1)
    o = jnp.einsum("brnhqk,brnkhd->brnqhd", p, vv.astype(jnp.float32))
    m = jnp.swapaxes(m, -1, -2)
    l = jnp.swapaxes(l, -1, -2)
    o = o / l[..., None]

    def merge(t):
        t = t.reshape((B, r, Lp) + t.shape[4:])[:, :, :L]
        return jnp.swapaxes(t, 1, 2).reshape((B, S) + t.shape[3:])

    return merge(o), merge(m), merge(l)


def dilated_attention(q, k, v, rel_bias):
    outs, maxes, dens = [], [], []
    for g, (window, dilation) in enumerate(DILATED_PATTERNS):
        hs = slice(g * HEADS_PER_GROUP, (g + 1) * HEADS_PER_GROUP)
        o, m, l = dilated_group_attention(q[:, :, hs], k[:, :, hs], v[:, :, hs],
                                          rel_bias[:, hs], window, dilation)
        outs.append(o); maxes.append(m); dens.append(l)
    o = jnp.stack(outs)
    m = jnp.stack(maxes)
    l = jnp.stack(dens)
    w = l * jnp.exp(m - jnp.max(m, axis=0, keepdims=True))
    return jnp.sum(w[..., None] * o, axis=0) / jnp.sum(w, axis=0)[..., None]


def setup_inputs(seed: int = 0) -> dict:
    key = jax.random.key(seed)
    ks = jax.random.split(key, 24)
    f32 = jnp.float32
    nrm = lambda k, shape, scale: jax.random.normal(k, shape, f32) * scale
    gain = lambda k, shape: 1.0 + 0.05 * jax.random.normal(k, shape, f32)
    return {
        "x": jax.random.normal(ks[0], (BATCH, SEQ, D_MODEL), f32),
        "w_in": nrm(ks[1], (DEPTH, D_MODEL, IN_WIDTH), D_MODEL ** -0.5),
        "b_gate": nrm(ks[2], (DEPTH, 2 * D_MODEL), 0.02),
        "rel_bias": nrm(ks[3], (N_BUCKETS, N_HEADS_A), 0.5),
        "w_attn_out": nrm(ks[4], (DEPTH, ATTN_OUT, D_MODEL), ATTN_OUT ** -0.5),
        "conv_dw_w": nrm(ks[5], (DEPTH, CONV_WIDTH, CONV_DIM), CONV_WIDTH ** -0.5),
        "conv_dw_b": nrm(ks[6], (DEPTH, CONV_DIM), 0.02),
        "conv_ln_g": gain(ks[7], (DEPTH, CONV_DIM)),
        "conv_ln_b": nrm(ks[8], (DEPTH, CONV_DIM), 0.02),
        "conv_pw_w": nrm(ks[9], (DEPTH, CONV_DIM, D_MODEL), CONV_DIM ** -0.5),
        "w_out": nrm(ks[10], (DEPTH, D_MODEL, D_MODEL), D_MODEL ** -0.5),
        "norm_mix_pre": gain(ks[11], (DEPTH, D_MODEL)),
        "norm_mix_post": gain(ks[12], (DEPTH, D_MODEL)),
        "norm_ffn_pre": gain(ks[13], (DEPTH, D_MODEL)),
        "norm_ffn_post": gain(ks[14], (DEPTH, D_MODEL)),
        "w_up": nrm(ks[15], (DEPTH, D_MODEL, 2 * D_FF), D_MODEL ** -0.5),
        "ffn_conv_w": nrm(ks[16], (DEPTH, FFN_CONV_WIDTH, 2 * D_FF), FFN_CONV_WIDTH ** -0.5),
        "ffn_conv_b": nrm(ks[17], (DEPTH, 2 * D_FF), 0.02),
        "w_down": nrm(ks[18], (DEPTH, D_FF, D_MODEL), D_FF ** -0.5),
    }


def reference(x, w_in, b_gate, rel_bias, w_attn_out, conv_dw_w, conv_dw_b, conv_ln_g,
              conv_ln_b, conv_pw_w, w_out, norm_mix_pre, norm_mix_post, norm_ffn_pre,
              norm_ffn_post, w_up, ffn_conv_w, ffn_conv_b, w_down):
    B, S, D = x.shape
    for layer in range(DEPTH):
        h = rms_norm(x, norm_mix_pre[layer])
        proj = h @ w_in[layer].astype(h.dtype)
        q, k, v, glu_in, gates = jnp.split(
            proj, np.cumsum([QKV_WIDTH, QKV_WIDTH, QKV_WIDTH, 2 * CONV_DIM]).tolist(), axis=-1)
        q = q.reshape(B, S, N_HEADS_A, HEAD_DIM)
        k = k.reshape(B, S, N_HEADS_A, HEAD_DIM)
        v = v.reshape(B, S, N_HEADS_A, HEAD_DIM)

        a = dilated_attention(q, k, v, rel_bias).reshape(B, S, ATTN_OUT).astype(x.dtype)
        y_a = a @ w_attn_out[layer].astype(x.dtype)

        c_val, c_gate = jnp.split(glu_in, 2, axis=-1)
        c = c_val * jax.nn.sigmoid(c_gate)
        c = causal_depthwise_conv(c, conv_dw_w[layer], conv_dw_b[layer])
        c = jax.nn.silu(layer_norm(c, conv_ln_g[layer], conv_ln_b[layer]))
        y_c = c @ conv_pw_w[layer].astype(x.dtype)

        g = jax.nn.sigmoid(gates + b_gate[layer].astype(gates.dtype))
        g_a, g_c = jnp.split(g, 2, axis=-1)
        mixed = g_a * y_a + g_c * y_c
        out = mixed @ w_out[layer].astype(x.dtype)
        x = x + rms_norm(out, norm_mix_post[layer])

        h = rms_norm(x, norm_ffn_pre[layer])
        u = h @ w_up[layer].astype(h.dtype)
        u = causal_depthwise_conv(u, ffn_conv_w[layer], ffn_conv_b[layer])
        u_gate, u_val = jnp.split(u, 2, axis=-1)
        f = jax.nn.gelu(u_gate, approximate=True) * u_val
        y = f @ w_down[layer].astype(f.dtype)
        x = x + rms_norm(y, norm_ffn_post[layer])
    return x
```

```python
import contextlib
import math
import numpy as np
import concourse.bass as bass
import concourse.mybir as mybir
from concourse.bass_utils import run_bass_kernel_spmd

F32 = mybir.dt.float32
BF16 = mybir.dt.bfloat16
AF = mybir.ActivationFunctionType
ALU = mybir.AluOpType

ENGS = ("pe", "act", "dve", "pool", "sp")


class TT:
    __slots__ = ("name", "w", "r")

    def __init__(self, name=""):
        self.name = name
        self.w = None
        self.r = {}


class Prog:
    def __init__(self, nc, stack):
        self.nc = nc
        self.stack = stack
        self.streams = {e: [] for e in ENGS}
        self.cnt = {}
        self.clock = {e: {} for e in ENGS}
        self.sems = {}
        for e in ENGS:
            self._sem(e)

    def _sem(self, key):
        if key not in self.sems:
            self.sems[key] = self.stack.enter_context(self.nc.semaphore("s_" + key))
            self.cnt[key] = 0
        return self.sems[key]

    def _deps(self, eng, reads, writes):
        clk = self.clock[eng]
        deps = []
        for t in reads:
            if t.w is not None:
                deps.append(t.w)
        for t in writes:
            if t.w is not None and t.w[3] != eng:
                deps.append(t.w)
            for ent in t.r.values():
                if ent[3] != eng:
                    deps.append(ent)
        need = {}
        for (sk, val, snap, weng) in deps:
            if clk.get(sk, 0) < val:
                need[sk] = max(need.get(sk, 0), val)
        for (sk, val, snap, weng) in deps:
            if snap:
                for k, v in snap.items():
                    if clk.get(k, 0) < v:
                        clk[k] = v
        waits = []
        for sk, val in need.items():
            if clk.get(sk, 0) < val:
                waits.append((sk, val))
                clk[sk] = val
        return waits

    def op(self, eng, fn, reads=(), writes=()):
        waits = self._deps(eng, reads, writes)
        self.cnt[eng] += 1
        tick = self.cnt[eng]
        snap = dict(self.clock[eng])
        self.streams[eng].append((waits, fn, (eng, 1)))
        ent = (eng, tick, snap, eng)
        for t in writes:
            t.w = ent
            t.r = {}
        for t in reads:
            t.r[eng] = ent

    def dma(self, queue, fn, semkey, reads=(), writes=()):
        waits = self._deps(queue, reads, writes)
        self._sem(semkey)
        self.cnt[semkey] += 16
        val = self.cnt[semkey]
        snap = dict(self.clock[queue])
        self.streams[queue].append((waits, fn, (semkey, 16)))
        ent = (semkey, val, snap, None)
        for t in writes:
            t.w = ent
            t.r = {}
        for t in reads:
            t.r[semkey] = ent
        return ent

    def barrier(self, engs=ENGS):
        for e in engs:
            clk = self.clock[e]
            waits = []
            for k, v in self.cnt.items():
                if k == e or v == 0:
                    continue
                if clk.get(k, 0) < v:
                    waits.append((k, v))
                    clk[k] = v
            if waits:
                self.streams[e].append((waits, None, None))

    def wait_all(self, eng, tiles):
        waits = self._deps(eng, tiles, ())
        if waits:
            self.streams[eng].append((waits, None, None))

    def emit(self):
        nc = self.nc
        with nc.Block() as block:
            def replay(e, name):
                for (waits, fn, inc) in self.streams[name]:
                    for (sk, val) in waits:
                        e.wait_ge(self.sems[sk], val)
                    if fn is None:
                        continue
                    ins = fn(e)
                    if inc is not None:
                        ins.then_inc(self.sems[inc[0]], inc[1])

            @block.tensor
            def _(e):
                replay(e, "pe")

            @block.scalar
            def _(e):
                replay(e, "act")

            @block.vector
            def _(e):
                replay(e, "dve")

            @block.gpsimd
            def _(e):
                replay(e, "pool")

            @block.sync
            def _(e):
                replay(e, "sp")


S = 2048
D = 1024
NH = 12
DH = 128
DFF = 2816
NFC = DFF // 128
INW = 8704
K0, V0, GLUV0, GLUG0, GA0, GC0 = 1536, 3072, 4608, 5632, 6656, 7680
DIL = (1, 4, 16)
NBLK = (16, 4, 1)
RMS_EPS = 1e-6
LN_EPS = 1e-5
MASKVAL = -30000.0

C_BG, C_DWW, C_DWB, C_LNG, C_LNB, C_FFW, C_FFB = 0, 16, 264, 272, 280, 288, 420
C_VEC = 464
C_GPRE_MIX = 464
C_GPRE_FFN = C_GPRE_MIX + 1024
C_BM = C_GPRE_FFN + 1024
C_GPOST_MIX = C_BM + 12 * 256
C_GPOST_FFN = C_GPOST_MIX + 1024
NCONST = C_GPOST_FFN + 1024

KB = 1024
BASE = 17 * KB


def bcast(ap, pos, n):
    l = [list(x) for x in ap.ap]
    l.insert(1 + pos, [0, n])
    return bass.AP(ap.tensor, ap.offset, l)


def build_program(stop=None, dbg=()):
    nc = bass.Bass("TRN2", target_bir_lowering=False)
    dram_in = lambda n, s: nc.dram_tensor(n, s, F32, kind="ExternalInput").ap()
    x = dram_in("x", [S, D])
    w_in = dram_in("w_in", [D, INW])
    w_ao = dram_in("w_ao", [512, D])
    w_pw = dram_in("w_pw", [D, D])
    w_out = dram_in("w_out", [D, D])
    w_up = dram_in("w_up", [D, 2 * DFF])
    w_down = dram_in("w_down", [DFF, D])
    consts = dram_in("consts", [128, NCONST])
    out = nc.dram_tensor("out", [S, D], F32, kind="ExternalOutput").ap()
    dbg_out = {}

    w_in_v = w_in.rearrange("(kc p) n -> p kc n", p=128)
    w_ao_v = w_ao.rearrange("(kc p) n -> p kc n", p=128)
    w_pw_v = w_pw.rearrange("(kc p) n -> p kc n", p=128)
    w_out_v = w_out.rearrange("(kc p) n -> p kc n", p=128)
    w_up_v = w_up.rearrange("(kc p) n -> p kc n", p=128)
    w_down_v = w_down.rearrange("(kc p) n -> p kc n", p=128)

    with contextlib.ExitStack() as st:
        P = Prog(nc, st)
        _n = [0]

        def sbt(off_kb, shape, dt):
            _n[0] += 1
            return nc.alloc_sbuf_tensor_at("t%d" % _n[0], shape, dt, offset=int(BASE + off_kb * KB))

        PS = [st.enter_context(nc.psum_tensor("ps%d" % i, [128, 1024], F32)) for i in range(4)]
        t_bank = [TT("bank%d" % i) for i in range(8)]
        bank = lambda b: PS[b // 2][:, (b % 2) * 512:(b % 2) * 512 + 512]
        bank_bf = lambda b: PS[b // 2][:, (b % 2) * 512:(b % 2) * 512 + 512].bitcast(BF16)

        cvec = sbt(0, [128, C_VEC], F32)
        gpre_mix = sbt(2, [128, 8, 128], F32)
        gpre_ffn = sbt(6, [128, 8, 128], F32)
        ident = sbt(10, [128, 128], BF16)
        ones_bf = sbt(10.25, [128, 128], BF16)
        ones_f = sbt(10.5, [128, 128], F32)
        stat = sbt(11, [128, 256], F32)
        ring = [sbt(12 + 8 * i, [128, 4096], BF16) for i in range(3)]
        t_ring = [TT("ring%d" % i) for i in range(3)]
        t_const = TT("const")
        t_ident = TT("ident")
        t_ones = TT("ones")

        def act(out_, in_, func, reads, writes, bias=None, scale=None, accum=None):
            kw = {}
            if bias is not None:
                kw["bias"] = bias
            if scale is not None:
                kw["scale"] = scale
            if accum is not None:
                kw["accum_out"] = accum
            P.op("act", lambda e: e.activation(out=out_, in_=in_, func=func, **kw), reads=reads, writes=writes)

        def tt_op(eng, out_, in0, in1, op, reads, writes):
            P.op(eng, lambda e: e.tensor_tensor(out=out_, in0=in0, in1=in1, op=op), reads=reads, writes=writes)

        def ts_op(eng, out_, in0, s1, s2, op0, op1, reads, writes):
            if s2 is None:
                P.op(eng, lambda e: e.tensor_scalar(out_, in0, s1, None, op0), reads=reads, writes=writes)
            else:
                P.op(eng, lambda e: e.tensor_scalar(out_, in0, s1, s2, op0, op1), reads=reads, writes=writes)

        def stt_op(eng, out_, in0, scalar, in1, op0, op1, reads, writes):
            P.op(eng, lambda e: e.scalar_tensor_tensor(out=out_, in0=in0, scalar=scalar, in1=in1, op0=op0, op1=op1),
                 reads=reads, writes=writes)

        def copy_op(eng, out_, in_, reads, writes):
            P.op(eng, lambda e: e.tensor_copy(out_, in_), reads=reads, writes=writes)

        def mm(groups, reads, writes, first=True, last=True):
            def fn(e):
                ins = None
                for (o, pairs) in groups:
                    n = len(pairs)
                    for i, (l, r) in enumerate(pairs):
                        ins = e.matmul(o, lhsT=l, rhs=r, start=(first and i == 0), stop=(last and i == n - 1))
                return ins
            P.op("pe", fn, reads=reads, writes=writes)

        def dma(queue, out_, in_, semkey, reads=(), writes=()):
            return P.dma(queue, lambda e: e.dma_start(out=out_, in_=in_), semkey, reads=reads, writes=writes)

        pieces = []

        def piece_cols(src_v, col0, width, kcs, off):
            return (lambda slot, off=off, kcs=kcs, width=width:
                    slot[:, off:off + kcs * width].rearrange("p (k n) -> p k n", k=kcs),
                    src_v[:, :, col0:col0 + width])

        for g in range(3):
            pieces.append([piece_cols(w_in_v, V0 + g * 512, 512, 8, 0)])
        for js in range(4):
            for g in range(3):
                h = 4 * g + js
                pieces.append([piece_cols(w_in_v, h * 128, 128, 8, 0),
                               piece_cols(w_in_v, K0 + h * 128, 128, 8, 1024)])
        for c in range(8):
            pieces.append([piece_cols(w_in_v, GLUV0 + c * 128, 128, 8, 0),
                           piece_cols(w_in_v, GLUG0 + c * 128, 128, 8, 1024)])
        for j in range(8):
            pieces.append([piece_cols(w_ao_v, j * 128, 128, 4, 0),
                           piece_cols(w_pw_v, j * 128, 128, 8, 512),
                           piece_cols(w_in_v, GA0 + j * 128, 128, 8, 1536),
                           piece_cols(w_in_v, GC0 + j * 128, 128, 8, 2560)])
        for m in range(NFC):
            pieces.append([piece_cols(w_up_v, m * 128, 128, 8, 0),
                           piece_cols(w_up_v, DFF + m * 128, 128, 8, 1024)])
        ring_state = {"issued": 0, "next": 0}

        def ring_issue_upto(k):
            while ring_state["issued"] <= k and ring_state["issued"] < len(pieces):
                i = ring_state["issued"]
                s = i % 3
                for (dst_fn, src) in pieces[i]:
                    dma("pool", dst_fn(ring[s]), src, "d_r%d" % s, writes=[t_ring[s]])
                ring_state["issued"] += 1

        def ring_next(ahead=2):
            k = ring_state["next"]
            ring_state["next"] += 1
            ring_issue_upto(k + ahead)
            return ring[k % 3], t_ring[k % 3]

        def ring_view(slot, off, kcs, width):
            return slot[:, off:off + kcs * width].rearrange("p (k n) -> p k n", k=kcs)

        def dump(name, sb_ap, shape, dt, treads):
            if name not in dbg:
                return
            d = nc.dram_tensor("dbg_" + name, shape, dt, kind="ExternalOutput").ap()
            dbg_out[name] = TT()
            dma("sp", d, sb_ap, "d_dbg", reads=treads, writes=[dbg_out[name]])

        def finish():
            fin = list(dbg_out.values()) + final_tts
            if fin:
                P.wait_all("sp", fin)
            P.emit()

        final_tts = []

        identf = sbt(178, [128, 128], F32)
        t_identf = TT()
        xs_all = sbt(84, [128, 16, D], F32)
        t_xs = [TT() for _ in range(16)]
        for i in range(2):
            dma("sp", xs_all[:, i, :], x[i * 128:(i + 1) * 128, :], "d_x%d" % i, writes=[t_xs[i]])
        t_gpre = TT()
        dma("sp", gpre_mix[:, :, :].rearrange("p a b -> p (a b)"), consts[:, C_GPRE_MIX:C_GPRE_MIX + 1024], "d_gpre", writes=[t_gpre])
        P.op("pool", lambda e: e.memset(identf[:, :], 0.0), writes=[t_identf])
        P.op("pool", lambda e: e.affine_select(out=identf[:, :], in_=identf[:, :], compare_op=ALU.not_equal, fill=1.0,
                                               base=0, pattern=[[-1, 128]], channel_multiplier=1),
             reads=[t_identf], writes=[t_identf])
        copy_op("dve", ident[:, :], identf[:, :], [t_identf], [t_ident])
        P.op("pool", lambda e: e.memset(ones_bf[:, :], 1.0), writes=[t_ones])
        P.op("pool", lambda e: e.memset(ones_f[:, :], 1.0), writes=[t_ones])
        P.op("pool", lambda e: e.memset(stat[:, 255:256], RMS_EPS), writes=[t_ones])
        P.op("pool", lambda e: e.memset(stat[:, 254:255], LN_EPS), writes=[t_ones])

        hT = sbt(36, [128, 8, S], BF16)
        t_hT = [TT("hT%d" % i) for i in range(16)]
        xn = [sbt(172 + 2 * i, [128, D], BF16) for i in range(2)]
        t_xn = [TT(), TT()]
        junk = sbt(176, [128, D], BF16)
        t_junk = TT()
        for i in range(2, 16):
            dma("sp", xs_all[:, i, :], x[i * 128:(i + 1) * 128, :], "d_x%d" % i,
                reads=([t_xs[i - 3]] if i >= 3 else []), writes=[t_xs[i]])
        dma("sp", cvec[:, :], consts[:, 0:C_VEC], "d_const", writes=[t_const])
        dma("sp", gpre_ffn[:, :, :].rearrange("p a b -> p (a b)"), consts[:, C_GPRE_FFN:C_GPRE_FFN + 1024], "d_const", writes=[t_const])
        P.wait_all("pool", [t_xs[9]])
        ring_issue_upto(1)
        t_pair = [TT() for _ in range(8)]

        def p1_stats(pr):
            ti = t_pair[pr]
            i0_ = 2 * pr
            for i in (i0_, i0_ + 1):
                act(junk[:, :], xs_all[:, i, :], AF.Square, [t_xs[i]], [t_junk, ti], accum=stat[:, i:i + 1])
            act(stat[:, 192 + i0_:194 + i0_], stat[:, i0_:i0_ + 2], AF.Ln, [ti, t_ones], [ti], bias=stat[:, 255:256], scale=1.0 / D)
            act(stat[:, 16 + i0_:18 + i0_], stat[:, 192 + i0_:194 + i0_], AF.Exp, [ti], [ti], scale=-0.5)

        def p1_scale(i):
            ti = t_pair[i // 2]
            s2 = i % 2
            if i % 4 == 3:
                act(xn[s2][:, :], xs_all[:, i, :], AF.Copy, [t_xs[i], ti], [t_xn[s2]], scale=stat[:, 16 + i:17 + i])
            else:
                ts_op("dve", xn[s2][:, :], xs_all[:, i, :], stat[:, 16 + i:17 + i], None, ALU.mult, None, [t_xs[i], ti], [t_xn[s2]])

        def p1_tr(i):
            s2 = i % 2
            b = i % 2
            tp = bank_bf(b)

            def tr_fn(e, s2=s2, tp=tp):
                ins = None
                for c in range(8):
                    ins = e.transpose(tp[:, c * 128:(c + 1) * 128], xn[s2][:, c * 128:(c + 1) * 128], ident[:, :])
                return ins
            P.op("pe", tr_fn, reads=[t_xn[s2], t_ident], writes=[t_bank[b]])
            tt_op("dve", hT[:, :, i * 128:(i + 1) * 128], tp[:, 0:1024].rearrange("p (c t) -> p c t", c=8),
                  gpre_mix[:, :, :], ALU.mult, [t_bank[b], t_gpre], [t_hT[i]])

        p1_stats(0)
        for step in range(17):
            if step < 16:
                if step % 2 == 0 and step // 2 + 1 < 8:
                    p1_stats(step // 2 + 1)
                p1_scale(step)
            if step >= 1:
                p1_tr(step - 1)
        dump("hT", hT[:, :, :].rearrange("p a b -> p (a b)"), [128, 8 * S], BF16, t_hT)
        if stop == 1:
            finish()
            return nc
        P.barrier(engs=("act", "dve", "pool", "sp"))

        Vall = sbt(84, [128, 3, 16, 512], BF16)
        t_V = [[TT() for _ in range(16)] for _ in range(3)]
        bm = sbt(132, [128, 12, 256], F32)
        t_bm = TT()
        dma("sp", bm[:, :, :].rearrange("p a b -> p (a b)"), consts[:, C_BM:C_BM + 3072], "d_const2", writes=[t_bm])

        def tok_tile(g, j):
            if g == 0:
                return slice(j * 128, (j + 1) * 128)
            if g == 1:
                r, b = j // 4, j % 4
                return slice(512 * b + r, 512 * (b + 1), 4)
            return slice(j, S, 16)

        nev = 0
        for g in range(3):
            slot, t_slot = ring_next()
            wv = ring_view(slot, 0, 8, 512)
            for j in range(16):
                b = 2 + (nev % 2)
                sl = tok_tile(g, j)
                mm([(bank(b), [(hT[:, c, sl], wv[:, c, :]) for c in range(8)])],
                   reads=t_hT + [t_slot], writes=[t_bank[b]])
                if nev % 2 == 0:
                    act(Vall[:, g, j, :], bank(b), AF.Copy, [t_bank[b]], [t_V[g][j]])
                else:
                    copy_op("dve", Vall[:, g, j, :], bank(b), [t_bank[b]], [t_V[g][j]])
                nev += 1
        dump("Vall", Vall[:, :, :, :].rearrange("p a b c -> p (a b c)"), [128, 3 * 16 * 512], BF16, [t for l in t_V for t in l])
        if stop == 2:
            finish()
            return nc

        aT = sbt(68, [128, 4, S], BF16)
        t_aT = [TT() for _ in range(4)]
        qT = [sbt(144 + 4 * i, [128, S], BF16) for i in range(2)]
        kT = [sbt(152 + 4 * i, [128, S], BF16) for i in range(2)]
        t_q = [TT(), TT()]
        t_k = [TT(), TT()]
        accO = sbt(160, [128, S], F32)
        accD = sbt(168, [128, S], F32)
        t_accO = TT()
        t_accD = TT()
        PT = [sbt(176 + 8 * i, [128, 16, 256], BF16) for i in range(2)]
        t_PT = [[TT() for _ in range(16)] for _ in range(2)]
        ssb = [sbt(192 + i, [128, 256], F32) for i in range(4)]
        t_ssb = [TT() for _ in range(4)]

        def acc_view(acc, g, bq):
            if g == 0:
                return acc[:, 512 * bq:512 * (bq + 1)]
            if g == 1:
                return acc[:, bq::4]
            return acc[:, :].rearrange("p (i r) -> p r i", r=16)[:, 4 * bq:4 * bq + 4, :]

        def ps_view(b, g):
            if g == 2:
                return bank(b).rearrange("p (r i) -> p r i", r=4)
            return bank(b)

        scale_q = 1.0 / math.sqrt(DH)
        heads = [(js, g) for js in range(4) for g in range(3)]
        t_sc = [TT() for _ in range(4)]
        nsb_box = [0]

        def proj_units(idx):
            js, g = heads[idx]
            h = 4 * g + js
            hs = idx % 2
            slot, t_slot = ring_next()
            wq = ring_view(slot, 0, 8, 128)
            wk = ring_view(slot, 1024, 8, 128)
            units = []
            for which in range(2):
                for tb in range(4):
                    def unit(which=which, tb=tb):
                        wsel = wq if which == 0 else wk
                        dst = qT[hs] if which == 0 else kT[hs]
                        b = (which * 4 + tb) % 2
                        mm([(bank(b), [(wsel[:, c, :], hT[:, c, 512 * tb:512 * (tb + 1)]) for c in range(8)])],
                           reads=t_hT[4 * tb:4 * tb + 4] + [t_slot], writes=[t_bank[b]])
                        if g == 0:
                            o_ap, i_ap = dst[:, 512 * tb:512 * (tb + 1)], bank(b)
                        else:
                            r = DIL[g]
                            w_ = 512 // r
                            o_ap = dst[:, :].rearrange("p (r n) -> p r n", r=r)[:, :, w_ * tb:w_ * (tb + 1)]
                            i_ap = bank(b).rearrange("p (j r) -> p r j", r=r)
                        if which == 0:
                            act(o_ap, i_ap, AF.Copy, [t_bank[b]], [t_q[hs]], scale=scale_q)
                        else:
                            act(o_ap, i_ap, AF.Copy, [t_bank[b]], [t_k[hs]])
                    units.append(unit)
            return units

        def score_tile(idx, kt):
            js, g = heads[idx]
            h = 4 * g + js
            hs = idx % 2
            nb = NBLK[g]
            bb = kt % nb
            ncols = 256 if bb + 1 < nb else 128
            sl = kt % 4
            b = 2 + sl // 2
            c0 = 256 * (sl % 2)
            tsl = t_sc[sl]
            mm([(bank(b)[:, c0:c0 + ncols], [(kT[hs][:, kt * 128:(kt + 1) * 128], qT[hs][:, kt * 128:kt * 128 + ncols])])],
               reads=[t_q[hs], t_k[hs]], writes=[tsl] + ([t_bank[b]] if (idx == 0 and kt < 4) else []))
            sb_i = nsb_box[0] % 4
            nsb_box[0] += 1
            tt_op("dve", ssb[sb_i][:, 0:ncols], bank(b)[:, c0:c0 + ncols], bm[:, h, 0:ncols], ALU.add,
                  [tsl, t_bm], [t_ssb[sb_i]])
            act(PT[hs][:, kt, 0:ncols], ssb[sb_i][:, 0:ncols], AF.Exp, [t_ssb[sb_i]], [t_PT[hs][kt]])

        def pv_head(idx):
            js, g = heads[idx]
            hs = idx % 2
            nb = NBLK[g]
            for bq in range(4):
                bo = 4 + 2 * (bq % 2)
                bd = bo + 1
                groups = []
                rd = [t_ones]
                for qi in range(4):
                    qb = 4 * bq + qi
                    pcs = [(qb, slice(0, 128))]
                    if qb % nb > 0:
                        pcs.append((qb - 1, slice(128, 256)))
                    groups.append((bank(bo)[:, qi * 128:(qi + 1) * 128],
                                   [(Vall[:, g, kt, js * 128:(js + 1) * 128], PT[hs][:, kt, cs]) for (kt, cs) in pcs]))
                    for (kt, cs) in pcs:
                        rd.append(t_PT[hs][kt])
                        rd.append(t_V[g][kt])
                q0 = 4 * bq
                dpairs = [(ones_bf[:, :], PT[hs][:, q0:q0 + 4, 0:128])]
                if nb > 1:
                    lo = 1 if (q0 % nb == 0) else 0
                    dpairs.append((ones_bf[:, :], PT[hs][:, q0 + lo - 1:q0 + 3, 128:256]))
                    if lo == 0 or q0 > 0:
                        rd.append(t_PT[hs][max(q0 - 1, 0)])

                def pv_fn(e, groups=groups, dpairs=dpairs, bd=bd, lo=(1 if (nb > 1 and q0 % nb == 0) else 0)):
                    ins = None
                    for (o, pairs) in groups:
                        n = len(pairs)
                        for i, (l, r) in enumerate(pairs):
                            ins = e.matmul(o, lhsT=l, rhs=r, start=(i == 0), stop=(i == n - 1))
                    ins = e.matmul(bank(bd)[:, 0:512], lhsT=dpairs[0][0], rhs=dpairs[0][1], start=True, stop=(len(dpairs) == 1))
                    if len(dpairs) > 1:
                        ins = e.matmul(bank(bd)[:, 128 * lo:512], lhsT=dpairs[1][0], rhs=dpairs[1][1], start=False, stop=True,
                                       skip_group_check=True)
                    return ins
                P.op("pe", pv_fn, reads=rd, writes=[t_bank[bo], t_bank[bd]])
                vo = acc_view(accO, g, bq)
                vd = acc_view(accD, g, bq)
                if g == 0:
                    act(vo, ps_view(bo, g), AF.Copy, [t_bank[bo]], [t_accO])
                    copy_op("dve", vd, ps_view(bd, g), [t_bank[bd]], [t_accD])
                else:
                    tt_op("dve", vo, ps_view(bo, g), vo, ALU.add, [t_bank[bo], t_accO], [t_accO])
                    tt_op("dve", vd, ps_view(bd, g), vd, ALU.add, [t_bank[bd], t_accD], [t_accD])

        def norm_piece(js, q):
            blk = slice(512 * q, 512 * (q + 1))
            act(accD[:, blk], accD[:, blk], AF.Ln, [t_accD], [t_accD])
            act(accD[:, blk], accD[:, blk], AF.Exp, [t_accD], [t_accD], scale=-1.0)
            tt_op("dve", aT[:, js, blk], accO[:, blk], accD[:, blk], ALU.mult, [t_accO, t_accD], [t_aT[js]])

        diag0 = sbt(196, [128, 31, 128], BF16)
        t_diag0 = TT()
        P.op("dve", lambda e: e.tensor_tensor(out=diag0[:, :, :], in0=bcast(ident[:, :], 0, 31),
                                              in1=bcast(cvec[:, C_DWW:C_DWW + 31], 1, 128), op=ALU.mult),
             reads=[t_ident, t_const], writes=[t_diag0])
        for u_ in proj_units(0):
            u_()
        pending = []
        for idx in range(len(heads)):
            nxt = proj_units(idx + 1) if idx + 1 < len(heads) else []
            for kt in range(16):
                score_tile(idx, kt)
                if kt % 2 == 1 and nxt:
                    nxt.pop(0)()
                if kt % 4 == 3 and pending:
                    norm_piece(*pending.pop(0))
            while nxt:
                nxt.pop(0)()
            pv_head(idx)
            if heads[idx][1] == 2:
                pending = [(heads[idx][0], q) for q in range(4)]
        for pc in pending:
            norm_piece(*pc)
        dump("aT", aT[:, :, :].rearrange("p a b -> p (a b)"), [128, 4 * S], BF16, t_aT)
        if stop == 3:
            finish()
            return nc
        ring_issue_upto(ring_state["next"] + 1)
        P.barrier(engs=("act", "dve", "pool", "sp"))

        csT = sbt(84, [128, 8, S], BF16)
        t_csT = [[TT() for _ in range(4)] for _ in range(8)]
        cc = sbt(116, [128, 8, S], BF16)
        t_cc = [[TT() for _ in range(4)] for _ in range(8)]
        acc1 = sbt(148, [128, S], F32)
        acc2 = sbt(156, [128, S], F32)
        t_acc1 = [TT() for _ in range(4)]
        t_acc2 = [TT() for _ in range(4)]
        cg = [sbt(164 + 4.25 * i, [128, 2176], BF16) for i in range(2)]
        t_cg = [[TT() for _ in range(5)] for _ in range(2)]
        diag = [diag0, sbt(172.5, [128, 31, 128], BF16)]
        t_diag = [t_diag0, TT()]
        sg = [sbt(188 + 2 * i, [128, 512], F32) for i in range(2)]
        t_sg = [TT(), TT()]
        sq = [sbt(192 + 2 * i, [128, 512], F32) for i in range(2)]
        t_sq = [TT(), TT()]
        for i in range(2):
            P.op("pool", (lambda i: lambda e: e.memset(cg[i][:, 0:30], 0.0))(i), writes=[t_cg[i][0]])
        nsg = 0
        nsq = 0
        for c in range(8):
            cb = c % 2
            slot, t_slot = ring_next()
            wval = ring_view(slot, 0, 8, 128)
            wgate = ring_view(slot, 1024, 8, 128)
            def build_diag(cn):
                P.op("dve", (lambda cb, c: lambda e: e.tensor_tensor(
                    out=diag[cb][:, :, :], in0=bcast(ident[:, :], 0, 31),
                    in1=bcast(cvec[:, C_DWW + c * 31:C_DWW + (c + 1) * 31], 1, 128), op=ALU.mult))(cn % 2, cn),
                    reads=[t_ident, t_const], writes=[t_diag[cn % 2]])
            for tb in range(4):
                bv, bg = 0 + 2 * (tb % 2), 1 + 2 * (tb % 2)
                mm([(bank(bv), [(wval[:, k, :], hT[:, k, 512 * tb:512 * (tb + 1)]) for k in range(8)]),
                    (bank(bg), [(wgate[:, k, :], hT[:, k, 512 * tb:512 * (tb + 1)]) for k in range(8)])],
                   reads=t_hT[4 * tb:4 * tb + 4] + [t_slot], writes=[t_bank[bv], t_bank[bg]])
                si = nsg % 2
                nsg += 1
                act(sg[si][:, :], bank(bg), AF.Sigmoid, [t_bank[bg]], [t_sg[si]])
                tt_op("dve", cg[cb][:, 30 + 512 * tb:30 + 512 * (tb + 1)], bank(bv), sg[si][:, :], ALU.mult,
                      [t_bank[bv], t_sg[si]], [t_cg[cb][1 + tb]])
            if c + 1 < 8:
                build_diag(c + 1)
            for tb in range(4):
                bc = 4 + (tb % 2)
                mm([(bank(bc), [(diag[cb][:, k, :], cg[cb][:, 512 * tb + k:512 * tb + k + 512]) for k in range(31)])],
                   reads=[t_diag[cb], t_cg[cb][tb], t_cg[cb][1 + tb]], writes=[t_bank[bc]])
                qi = nsq % 2
                nsq += 1
                blk = slice(512 * tb, 512 * (tb + 1))
                act(sg[qi][:, :], bank(bc), AF.Identity, [t_bank[bc]], [t_sg[qi]], bias=cvec[:, C_DWB + c:C_DWB + c + 1])
                copy_op("pool", cc[:, c, blk], sg[qi][:, :], [t_sg[qi]], [t_cc[c][tb]])
                act(sq[qi][:, :], sg[qi][:, :], AF.Square, [t_sg[qi]], [t_sq[qi]])
                if c == 0:
                    copy_op("dve", acc1[:, blk], sg[qi][:, :], [t_sg[qi]], [t_acc1[tb]])
                    copy_op("pool", acc2[:, blk], sq[qi][:, :], [t_sq[qi]], [t_acc2[tb]])
                else:
                    tt_op("dve", acc1[:, blk], sg[qi][:, :], acc1[:, blk], ALU.add, [t_sg[qi], t_acc1[tb]], [t_acc1[tb]])
                    tt_op("pool", acc2[:, blk], sq[qi][:, :], acc2[:, blk], ALU.add, [t_sq[qi], t_acc2[tb]], [t_acc2[tb]])
        dump("cc", cc[:, :, :].rearrange("p a b -> p (a b)"), [128, 8 * S], BF16, [t for l in t_cc for t in l])
        def ln_stats(tb):
            blk = slice(512 * tb, 512 * (tb + 1))
            b1, b2 = 0 + 2 * (tb % 2), 1 + 2 * (tb % 2)
            mm([(bank(b1), [(ones_f[:, :], acc1[:, blk])]), (bank(b2), [(ones_f[:, :], acc2[:, blk])])],
               reads=[t_ones, t_acc1[tb], t_acc2[tb]], writes=[t_bank[b1], t_bank[b2]])
            act(acc1[:, blk], bank(b1), AF.Copy, [t_bank[b1]], [t_acc1[tb]], scale=1.0 / D)
            qi = tb % 2
            act(sq[qi][:, :], acc1[:, blk], AF.Square, [t_acc1[tb]], [t_sq[qi]])
            stt_op("dve", acc2[:, blk], bank(b2), 1.0 / D, sq[qi][:, :], ALU.mult, ALU.subtract, [t_bank[b2], t_sq[qi]], [t_acc2[tb]])
            act(acc2[:, blk], acc2[:, blk], AF.Ln, [t_acc2[tb], t_ones], [t_acc2[tb]], bias=stat[:, 254:255])
            act(acc2[:, blk], acc2[:, blk], AF.Exp, [t_acc2[tb]], [t_acc2[tb]], scale=-0.5)
            act(negm[:, blk], acc1[:, blk], AF.Copy, [t_acc1[tb]], [t_negm[tb]], scale=-1.0)
        nz_box = [0]
        negm = sbt(180.5, [128, S], BF16)
        t_negm = [TT() for _ in range(4)]
        ztmp = [sg[0], sg[1], sq[0], sq[1]]
        t_ztmp = [t_sg[0], t_sg[1], t_sq[0], t_sq[1]]

        def ln_apply(tb):
            blk = slice(512 * tb, 512 * (tb + 1))
            for c in range(8):
                zi = nz_box[0] % 4
                bl = 4 + (nz_box[0] % 4)
                nz_box[0] += 1
                mm([(bank(bl), [(ident[:, :], cc[:, c, blk]), (ident[:, :], negm[:, blk])])],
                   reads=[t_ident, t_cc[c][tb], t_negm[tb]], writes=[t_bank[bl]])
                tt_op("dve", ztmp[zi][:, :], bank(bl), acc2[:, blk], ALU.mult, [t_bank[bl], t_acc2[tb]], [t_ztmp[zi]])
                act(csT[:, c, blk], ztmp[zi][:, :], AF.Silu, [t_ztmp[zi], t_const], [t_csT[c][tb]],
                    bias=cvec[:, C_LNB + c:C_LNB + c + 1], scale=cvec[:, C_LNG + c:C_LNG + c + 1])

        if stop == 4:
            for tb in range(4):
                ln_stats(tb)
            for tb in range(4):
                ln_apply(tb)
            dump("csT", csT[:, :, :].rearrange("p a b -> p (a b)"), [128, 8 * S], BF16, [t for l in t_csT for t in l])
            finish()
            return nc
        ring_issue_upto(ring_state["next"] + 1)

        mixT = sbt(116, [128, 8, S], BF16)
        t_mix = t_cc
        sga = [sbt(164 + 2 * i, [128, 512], F32) for i in range(2)]
        sgc = [sbt(168 + 2 * i, [128, 512], F32) for i in range(2)]
        m1 = [sbt(172 + 2 * i, [128, 512], F32) for i in range(2)]
        m2 = [sbt(176 + 2 * i, [128, 512], F32) for i in range(2)]
        t_sga, t_sgc, t_m1, t_m2 = [TT(), TT()], [TT(), TT()], [TT(), TT()], [TT(), TT()]
        Wout = sbt(180, [128, 8, D], BF16)
        t_Wout = TT()
        it_box = [0]
        ln_window = [True]
        slots5 = {}

        def blk5(j, tb):
            if j not in slots5:
                if j == 2:
                    for nbk in range(2):
                        dma("pool", Wout[:, :, 512 * nbk:512 * (nbk + 1)], w_out_v[:, :, 512 * nbk:512 * (nbk + 1)], "d_wout",
                            writes=[t_Wout, t_sg[0], t_sg[1], t_sq[0], t_sq[1], t_diag[1]] + t_negm)
                slots5[j] = ring_next(ahead=(1 if j == 1 else 2))
            slot, t_slot = slots5[j]
            wao = ring_view(slot, 0, 4, 128)
            wpw = ring_view(slot, 512, 8, 128)
            wga = ring_view(slot, 1536, 8, 128)
            wgc = ring_view(slot, 2560, 8, 128)
            blk = slice(512 * tb, 512 * (tb + 1))
            p = 0 if ln_window[0] else it_box[0] % 2
            it_box[0] += 1
            bya, byc, bga, bgc = 4 * p, 4 * p + 1, 4 * p + 2, 4 * p + 3
            mm([(bank(bga), [(wga[:, k, :], hT[:, k, blk]) for k in range(8)]),
                (bank(bgc), [(wgc[:, k, :], hT[:, k, blk]) for k in range(8)]),
                (bank(bya), [(wao[:, k, :], aT[:, k, blk]) for k in range(4)])],
               reads=t_hT[4 * tb:4 * tb + 4] + t_aT + [t_slot],
               writes=[t_bank[bya], t_bank[bga], t_bank[bgc]])
            mm([(bank(byc), [(wpw[:, k, :], csT[:, k, blk]) for k in range(8)])],
               reads=[t_csT[k][tb] for k in range(8)] + [t_slot], writes=[t_bank[byc]])
            act(sga[p][:, :], bank(bga), AF.Sigmoid, [t_bank[bga], t_const], [t_sga[p]], bias=cvec[:, C_BG + j:C_BG + j + 1])
            act(sgc[p][:, :], bank(bgc), AF.Sigmoid, [t_bank[bgc], t_const], [t_sgc[p]], bias=cvec[:, C_BG + 8 + j:C_BG + 9 + j])
            tt_op("dve", m1[p][:, :], bank(bya), sga[p][:, :], ALU.mult, [t_bank[bya], t_sga[p]], [t_m1[p]])
            tt_op("dve", m2[p][:, :], bank(byc), sgc[p][:, :], ALU.mult, [t_bank[byc], t_sgc[p]], [t_m2[p]])
            tt_op("dve", mixT[:, j, blk], m1[p][:, :], m2[p][:, :], ALU.add, [t_m1[p], t_m2[p]], [t_mix[j][tb]])

        seq5 = [("st", 0), ("st", 1), ("st", 2), ("st", 3), ("ln", 0), ("ln", 1), ("b", 0, 0), ("ln", 2), ("b", 0, 1), ("b", 1, 0),
                ("ln", 3), ("b", 0, 2), ("b", 1, 1), ("b", 0, 3), ("b", 1, 2), ("b", 1, 3)]
        seq5 += [("b", j, tb) for j in range(2, 8) for tb in range(4)]
        for item in seq5:
            if item[0] == "st":
                ln_stats(item[1])
            elif item[0] == "ln":
                ln_apply(item[1])
                if item[1] == 3:
                    ln_window[0] = False
            else:
                blk5(item[1], item[2])
        dump("mixT", mixT[:, :, :].rearrange("p a b -> p (a b)"), [128, 8 * S], BF16, [t for l in t_mix for t in l])
        if stop == 5:
            finish()
            return nc
        ring_issue_upto(ring_state["next"] + 1)
        P.barrier(engs=("act", "dve", "pool", "sp"))

        h2T = sbt(36, [128, 8, S], BF16)
        t_h2T = [TT() for _ in range(16)]
        NS6 = 4
        xs = [sbt(84 + 4 * i, [128, D], F32) for i in range(NS6)]
        t_xs2 = [TT() for _ in range(NS6)]
        x1s = [sbt(100 + 4 * i, [128, D], F32) for i in range(NS6)]
        t_x1s = [TT() for _ in range(NS6)]
        xn2 = [sbt(196 + 2 * i, [128, D], BF16) for i in range(2)]
        t_xn2 = [TT(), TT()]
        junk2 = sbt(200, [128, D], BF16)
        t_junk2 = TT()
        gpost = sbt(148, [128, D], F32)
        t_gpost = TT()
        WdA = sbt(156, [128, 11, D], BF16)
        t_WdA = TT()
        dma("sp", gpost[:, :], consts[:, C_GPOST_MIX:C_GPOST_MIX + 1024], "d_gp", writes=[t_gpost])
        dma("pool", WdA[:, :, :], w_down_v[:, 0:11, :], "d_wda", writes=[t_WdA])
        t_x1d = [TT() for _ in range(16)]
        t_s2 = [TT() for _ in range(16)]
        t_s3 = [TT() for _ in range(16)]
        eps_ap = stat[:, 255:256]

        def act_rstd(col_ss, col_tmp, col_out, t):
            act(col_tmp, col_ss, AF.Ln, [t, t_ones], [t], bias=eps_ap, scale=1.0 / D)
            act(col_out, col_tmp, AF.Exp, [t], [t], scale=-0.5)

        def p6_A(i):
            s3 = i % NS6
            dma("sp", xs[s3][:, :], x[i * 128:(i + 1) * 128, :], "d_xs%d" % s3, writes=[t_xs2[s3]])
            pp = i % 3
            b0, b1 = 2 * pp, 2 * pp + 1
            mm([(bank(b0), [(mixT[:, k, i * 128:(i + 1) * 128], Wout[:, k, 0:512]) for k in range(8)]),
                (bank(b1), [(mixT[:, k, i * 128:(i + 1) * 128], Wout[:, k, 512:1024]) for k in range(8)])],
               reads=[t_mix[k][i // 4] for k in range(8)] + [t_Wout], writes=[t_bank[b0], t_bank[b1]])
            act(junk2[:, :], PS[pp][:, :], AF.Square, [t_bank[b0], t_bank[b1]], [t_junk2, t_s2[i]], accum=stat[:, 32 + i:33 + i])
            act_rstd(stat[:, 32 + i:33 + i], stat[:, 128 + i:129 + i], stat[:, 48 + i:49 + i], t_s2[i])

        def p6_B1(i):
            s3 = i % NS6
            pp = i % 3
            b0, b1 = 2 * pp, 2 * pp + 1
            stt_op("dve", x1s[s3][:, :], PS[pp][:, :], stat[:, 48 + i:49 + i], gpost[:, :], ALU.mult, ALU.mult,
                   [t_bank[b0], t_bank[b1], t_s2[i], t_gpost], [t_x1s[s3]])
            tt_op("dve", x1s[s3][:, :], x1s[s3][:, :], xs[s3][:, :], ALU.add, [t_x1s[s3], t_xs2[s3]], [t_x1s[s3]])
            dma("sp", out[i * 128:(i + 1) * 128, :], x1s[s3][:, :], "d_x1w", reads=[t_x1s[s3]], writes=[t_x1d[i]])

        def p6_B2(i):
            s3 = i % NS6
            act(junk2[:, :], x1s[s3][:, :], AF.Square, [t_x1s[s3]], [t_junk2, t_s3[i]], accum=stat[:, 64 + i:65 + i])
            act_rstd(stat[:, 64 + i:65 + i], stat[:, 144 + i:145 + i], stat[:, 80 + i:81 + i], t_s3[i])
            s2 = i % 2
            if i % 2 == 0:
                act(xn2[s2][:, :], x1s[s3][:, :], AF.Copy, [t_x1s[s3], t_s3[i]], [t_xn2[s2]], scale=stat[:, 80 + i:81 + i])
            else:
                ts_op("dve", xn2[s2][:, :], x1s[s3][:, :], stat[:, 80 + i:81 + i], None, ALU.mult, None,
                      [t_x1s[s3], t_s3[i]], [t_xn2[s2]])

        def p6_C(i):
            s2 = i % 2
            bt = 6 + (i % 2)
            tp = bank_bf(bt)

            def tr_fn2(e, s2=s2, tp=tp):
                ins = None
                for c in range(8):
                    ins = e.transpose(tp[:, c * 128:(c + 1) * 128], xn2[s2][:, c * 128:(c + 1) * 128], ident[:, :])
                return ins
            P.op("pe", tr_fn2, reads=[t_xn2[s2], t_ident], writes=[t_bank[bt]])
            tt_op("dve", h2T[:, :, i * 128:(i + 1) * 128], tp[:, 0:1024].rearrange("p (c t) -> p c t", c=8),
                  gpre_ffn[:, :, :], ALU.mult, [t_bank[bt], t_const], [t_h2T[i]])

        for it in range(16 + 3):
            if it < 16:
                p6_A(it)
            if 0 <= it - 1 < 16:
                p6_B1(it - 1)
            if 0 <= it - 2 < 16:
                p6_B2(it - 2)
            if 0 <= it - 3 < 16:
                p6_C(it - 3)
        for t in t_x1d:
            t.w = t_x1d[-1].w
        dump("h2T", h2T[:, :, :].rearrange("p a b -> p (a b)"), [128, 8 * S], BF16, t_h2T)
        if stop == 6:
            final_tts.extend(t_x1d)
            finish()
            return nc
        ring_issue_upto(ring_state["next"] + 1)
        P.barrier(engs=("act", "dve", "pool", "sp"))

        fT = sbt(68, [128, NFC, S], BF16)
        t_fT = [[TT() for _ in range(4)] for _ in range(NFC)]
        NS8 = 3
        ubw = [sbt(178 + 8.25 * w, [128, 2050], F32) for w in range(2)]
        t_ubb = [[TT() for _ in range(4)] for _ in range(2)]
        t_ubpad = [TT(), TT()]
        t0 = [[sbt(194.5 + 2 * (NS8 * w + i), [128, 512], F32) for i in range(NS8)] for w in range(2)]
        t_t0 = [[TT() for _ in range(NS8)] for _ in range(2)]
        slots8 = {}
        for w in range(2):
            P.op("pool", (lambda w: lambda e: e.memset(ubw[w][:, 0:2], 0.0))(w), writes=[t_ubpad[w]])

        def ub_prev(w, tb):
            return t_ubb[w][tb - 1] if tb > 0 else t_ubpad[w]

        def p8_s1(n):
            m, tb = n // 4, n % 4
            if tb == 0:
                slot, t_slot = ring_next()
                slots8[m] = (slot, t_slot)
            slot, t_slot = slots8[m]
            wsel = [ring_view(slot, 0, 8, 128), ring_view(slot, 1024, 8, 128)]
            blk = slice(512 * tb, 512 * (tb + 1))
            u = n % NS8
            bks = [(2 * n) % 6, (2 * n + 1) % 6]
            for w in range(2):
                mm([(bank(bks[w]), [(wsel[w][:, k, :], h2T[:, k, blk]) for k in range(8)])],
                   reads=t_h2T[4 * tb:4 * tb + 4] + [t_slot], writes=[t_bank[bks[w]]])
            for w in range(2):
                act(ubw[w][:, 2 + 512 * tb:2 + 512 * (tb + 1)], bank(bks[w]), AF.Copy, [t_bank[bks[w]]], [t_ubb[w][tb]])
            for w in range(2):
                ch = m if w == 0 else NFC + m
                fw2 = cvec[:, C_FFW + 3 * ch + 2:C_FFW + 3 * ch + 3]
                fb = cvec[:, C_FFB + ch:C_FFB + ch + 1]
                act(t0[w][u][:, :], ubw[w][:, 2 + 512 * tb:2 + 512 * (tb + 1)], AF.Identity, [t_ubb[w][tb], t_const], [t_t0[w][u]],
                    bias=fb, scale=fw2)

        def p8_s2(n):
            m, tb = n // 4, n % 4
            u = n % NS8
            for w in range(2):
                ch = m if w == 0 else NFC + m
                fw1 = cvec[:, C_FFW + 3 * ch + 1:C_FFW + 3 * ch + 2]
                stt_op("dve", t0[w][u][:, :], ubw[w][:, 1 + 512 * tb:1 + 512 * (tb + 1)], fw1, t0[w][u][:, :], ALU.mult, ALU.add,
                       [t_ubb[w][tb], ub_prev(w, tb), t_t0[w][u]], [t_t0[w][u]])
            for w in range(2):
                ch = m if w == 0 else NFC + m
                fw0 = cvec[:, C_FFW + 3 * ch + 0:C_FFW + 3 * ch + 1]
                stt_op("dve", t0[w][u][:, :], ubw[w][:, 512 * tb:512 * (tb + 1)], fw0, t0[w][u][:, :], ALU.mult, ALU.add,
                       [t_ubb[w][tb], ub_prev(w, tb), t_t0[w][u]], [t_t0[w][u]])

        def p8_s3a(n):
            u = n % NS8
            act(t0[0][u][:, :], t0[0][u][:, :], AF.Gelu_apprx_tanh, [t_t0[0][u]], [t_t0[0][u]])

        def p8_s3b(n):
            m, tb = n // 4, n % 4
            blk = slice(512 * tb, 512 * (tb + 1))
            u = n % NS8
            tt_op("dve", fT[:, m, blk], t0[0][u][:, :], t0[1][u][:, :], ALU.mult, [t_t0[0][u], t_t0[1][u]], [t_fT[m][tb]])

        NB8 = NFC * 4
        for it in range(NB8 + 2):
            if 0 <= it - 2 < NB8:
                p8_s3a(it - 2)
            if it < NB8:
                p8_s1(it)
            if 0 <= it - 1 < NB8:
                p8_s2(it - 1)
            if 0 <= it - 2 < NB8:
                p8_s3b(it - 2)
        dump("fT", fT[:, :, :].rearrange("p a b -> p (a b)"), [128, NFC * S], BF16, [t for l in t_fT for t in l])
        if stop == 8:
            final_tts.extend(t_x1d)
            finish()
            return nc

        WdB = sbt(36, [128, 11, D], BF16)
        t_WdB = TT()
        dma("pool", WdB[:, :, :], w_down_v[:, 11:22, :], "d_wdb", writes=[t_WdB] + t_h2T)
        old8 = [t for l in t_ubb for t in l] + t_ubpad + [t for l in t_t0 for t in l]
        gpost2 = sbt(194, [128, D], F32)
        t_gpost2 = TT()
        dma("sp", gpost2[:, :], consts[:, C_GPOST_FFN:C_GPOST_FFN + 1024], "d_gp", writes=[t_gpost2] + old8)
        x1r = [sbt(178 + 4 * i, [128, D], F32) for i in range(2)]
        t_x1r = [TT(), TT()]
        osl = [sbt(186 + 4 * i, [128, D], F32) for i in range(2)]
        t_osl = [TT(), TT()]
        junk3 = sbt(58, [128, D], BF16)
        t_junk3 = TT()
        t_s4 = [TT() for _ in range(16)]
        t_outd = []

        def wd(mi, cs):
            return WdA[:, mi, cs] if mi < 11 else WdB[:, mi - 11, cs]

        def pf_mm(i, part):
            pp = (i + 3) % 4
            b0, b1 = 2 * pp, 2 * pp + 1
            rng = range(NFC) if part is None else (range(0, 11) if part == 0 else range(11, NFC))
            rd = [t_fT[mi][i // 4] for mi in rng] + ([t_WdA] if part in (None, 0) else []) + ([t_WdB] if part in (None, 1) else [])
            mm([(bank(b0), [(fT[:, mi, i * 128:(i + 1) * 128], wd(mi, slice(0, 512))) for mi in rng]),
                (bank(b1), [(fT[:, mi, i * 128:(i + 1) * 128], wd(mi, slice(512, 1024))) for mi in rng])],
               reads=rd, writes=[t_bank[b0], t_bank[b1]], first=(part in (None, 0)), last=(part in (None, 1)))

        def pf_A(i):
            s2 = i % 2
            extra = (old8 if i < 2 else [])
            dma("sp", x1r[s2][:, :], out[i * 128:(i + 1) * 128, :], "d_x1r%d" % s2, reads=[t_x1d[i]], writes=[t_x1r[s2]] + extra)
            pp = (i + 3) % 4
            b0, b1 = 2 * pp, 2 * pp + 1
            act(junk3[:, :], PS[pp][:, :], AF.Square, [t_bank[b0], t_bank[b1]], [t_junk3, t_s4[i]] + (t_h2T if i == 0 else []),
                accum=stat[:, 96 + i:97 + i])
            act_rstd(stat[:, 96 + i:97 + i], stat[:, 160 + i:161 + i], stat[:, 112 + i:113 + i], t_s4[i])

        def pf_B(i):
            s2 = i % 2
            pp = (i + 3) % 4
            b0, b1 = 2 * pp, 2 * pp + 1
            extra = (old8 if i < 2 else [])
            stt_op("dve", osl[s2][:, :], PS[pp][:, :], stat[:, 112 + i:113 + i], gpost2[:, :], ALU.mult, ALU.mult,
                   [t_bank[b0], t_bank[b1], t_s4[i], t_gpost2], [t_osl[s2]] + extra)
            tt_op("dve", osl[s2][:, :], osl[s2][:, :], x1r[s2][:, :], ALU.add, [t_osl[s2], t_x1r[s2]], [t_osl[s2]])
            tt = TT()
            dma("sp", out[i * 128:(i + 1) * 128, :], osl[s2][:, :], "d_out", reads=[t_osl[s2]], writes=[tt])
            t_outd.append(tt)

        for i in range(4):
            pf_mm(i, 0)
        for i in range(4):
            pf_mm(i, 1)
            pf_A(i)
            if i >= 1:
                pf_B(i - 1)
        for it in range(4, 16 + 1):
            if it < 16:
                pf_mm(it, None)
                pf_A(it)
            if 0 <= it - 1 < 16:
                pf_B(it - 1)
        for t in t_outd:
            t.w = t_outd[-1].w
        final_tts.extend(t_outd)
        finish()
    return nc


def _t5_bucket_np(dist):
    max_exact = 16
    d = np.maximum(dist, 1).astype(np.float32)
    val = np.log(d / np.float32(max_exact)) / np.float32(math.log(2048 / max_exact)) * np.float32(32 - max_exact)
    large = max_exact + val.astype(np.int32)
    large = np.minimum(large, 31)
    return np.where(dist < max_exact, dist, large)


def _vec_layout(v, chunks):
    return np.ascontiguousarray(np.asarray(v, np.float32).reshape(chunks, 128).T)


def make_consts(inp):
    c = np.zeros((128, NCONST), np.float32)
    c[:, C_BG:C_BG + 16] = _vec_layout(inp["b_gate"][0], 16)
    dww = np.asarray(inp["conv_dw_w"][0], np.float32)
    c[:, C_DWW:C_DWW + 248] = dww.reshape(31, 8, 128).transpose(2, 1, 0).reshape(128, 248)
    c[:, C_DWB:C_DWB + 8] = _vec_layout(inp["conv_dw_b"][0], 8)
    c[:, C_LNG:C_LNG + 8] = _vec_layout(inp["conv_ln_g"][0], 8)
    c[:, C_LNB:C_LNB + 8] = _vec_layout(inp["conv_ln_b"][0], 8)
    ffw = np.asarray(inp["ffn_conv_w"][0], np.float32)
    c[:, C_FFW:C_FFW + 132] = ffw.reshape(3, 44, 128).transpose(2, 1, 0).reshape(128, 132)
    c[:, C_FFB:C_FFB + 44] = _vec_layout(inp["ffn_conv_b"][0], 44)
    gm = _vec_layout(inp["norm_mix_pre"][0], 8)
    c[:, C_GPRE_MIX:C_GPRE_MIX + 1024] = np.repeat(gm[:, :, None], 128, axis=2).reshape(128, 1024)
    gf = _vec_layout(inp["norm_ffn_pre"][0], 8)
    c[:, C_GPRE_FFN:C_GPRE_FFN + 1024] = np.repeat(gf[:, :, None], 128, axis=2).reshape(128, 1024)
    rb = np.asarray(inp["rel_bias"], np.float32)
    p = np.arange(128)[:, None]
    qq = np.arange(256)[None, :]
    rel = qq - p
    valid = (rel >= 0) & (rel <= 128)
    bmx = np.zeros((128, 12, 256), np.float32)
    for h in range(12):
        r = DIL[h // 4]
        bucket = _t5_bucket_np(np.maximum(rel, 0) * r)
        bmx[:, h, :] = np.where(valid, rb[bucket, h], np.float32(MASKVAL))
    c[:, C_BM:C_BM + 3072] = bmx.reshape(128, 3072)
    c[:, C_GPOST_MIX:C_GPOST_MIX + 1024] = np.broadcast_to(np.asarray(inp["norm_mix_post"][0], np.float32)[None, :], (128, 1024))
    c[:, C_GPOST_FFN:C_GPOST_FFN + 1024] = np.broadcast_to(np.asarray(inp["norm_ffn_post"][0], np.float32)[None, :], (128, 1024))
    return c


def make_in_maps(inp, n_cores=8):
    f = lambda a: np.ascontiguousarray(np.asarray(a, np.float32))
    consts = make_consts(inp)
    shared = {
        "w_in": f(inp["w_in"][0]), "w_ao": f(inp["w_attn_out"][0]), "w_pw": f(inp["conv_pw_w"][0]),
        "w_out": f(inp["w_out"][0]), "w_up": f(inp["w_up"][0]), "w_down": f(inp["w_down"][0]),
        "consts": consts,
    }
    xs = f(inp["x"])
    return [dict(shared, x=xs[b]) for b in range(n_cores)]


_NC_CACHE = {}


def kernel(**inputs):
    if "nc" not in _NC_CACHE:
        _NC_CACHE["nc"] = build_program()
    nc = _NC_CACHE["nc"]
    in_maps = make_in_maps(inputs)
    res = run_bass_kernel_spmd(nc, in_maps, core_ids=list(range(8)))
    return np.stack([np.asarray(r["out"], np.float32) for r in res.results], axis=0)
```

```python
import contextlib
import math
import numpy as np
import concourse.bass as bass
import concourse.mybir as mybir
from concourse.bass_utils import run_bass_kernel_spmd

F32 = mybir.dt.float32
BF16 = mybir.dt.bfloat16
AF = mybir.ActivationFunctionType
ALU = mybir.AluOpType

ENGS = ("pe", "act", "dve", "pool", "sp")


class TT:
    __slots__ = ("name", "w", "r")

    def __init__(self, name=""):
        self.name = name
        self.w = None
        self.r = {}


class Prog:
    def __init__(self, nc, stack):
        self.nc = nc
        self.stack = stack
        self.streams = {e: [] for e in ENGS}
        self.cnt = {}
        self.clock = {e: {} for e in ENGS}
        self.sems = {}
        for e in ENGS:
            self._sem(e)

    def _sem(self, key):
        if key not in self.sems:
            self.sems[key] = self.stack.enter_context(self.nc.semaphore("s_" + key))
            self.cnt[key] = 0
        return self.sems[key]

    def _deps(self, eng, reads, writes):
        clk = self.clock[eng]
        deps = []
        for t in reads:
            if t.w is not None:
                deps.append(t.w)
        for t in writes:
            if t.w is not None and t.w[3] != eng:
                deps.append(t.w)
            for ent in t.r.values():
                if ent[3] != eng:
                    deps.append(ent)
        need = {}
        for (sk, val, snap, weng) in deps:
            if clk.get(sk, 0) < val:
                need[sk] = max(need.get(sk, 0), val)
        for (sk, val, snap, weng) in deps:
            if snap:
                for k, v in snap.items():
                    if clk.get(k, 0) < v:
                        clk[k] = v
        waits = []
        for sk, val in need.items():
            if clk.get(sk, 0) < val:
                waits.append((sk, val))
                clk[sk] = val
        return waits

    def op(self, eng, fn, reads=(), writes=()):
        waits = self._deps(eng, reads, writes)
        self.cnt[eng] += 1
        tick = self.cnt[eng]
        snap = dict(self.clock[eng])
        self.streams[eng].append((waits, fn, (eng, 1)))
        ent = (eng, tick, snap, eng)
        for t in writes:
            t.w = ent
            t.r = {}
        for t in reads:
            t.r[eng] = ent

    def dma(self, queue, fn, semkey, reads=(), writes=()):
        waits = self._deps(queue, reads, writes)
        self._sem(semkey)
        self.cnt[semkey] += 16
        val = self.cnt[semkey]
        snap = dict(self.clock[queue])
        self.streams[queue].append((waits, fn, (semkey, 16)))
        ent = (semkey, val, snap, None)
        for t in writes:
            t.w = ent
            t.r = {}
        for t in reads:
            t.r[semkey] = ent
        return ent

    def barrier(self, engs=ENGS):
        for e in engs:
            clk = self.clock[e]
            waits = []
            for k, v in self.cnt.items():
                if k == e or v == 0:
                    continue
                if clk.get(k, 0) < v:
                    waits.append((k, v))
                    clk[k] = v
            if waits:
                self.streams[e].append((waits, None, None))

    def wait_all(self, eng, tiles):
        waits = self._deps(eng, tiles, ())
        if waits:
            self.streams[eng].append((waits, None, None))

    def emit(self):
        nc = self.nc
        with nc.Block() as block:
            def replay(e, name):
                for (waits, fn, inc) in self.streams[name]:
                    for (sk, val) in waits:
                        e.wait_ge(self.sems[sk], val)
                    if fn is None:
                        continue
                    ins = fn(e)
                    if inc is not None:
                        ins.then_inc(self.sems[inc[0]], inc[1])

            @block.tensor
            def _(e):
                replay(e, "pe")

            @block.scalar
            def _(e):
                replay(e, "act")

            @block.vector
            def _(e):
                replay(e, "dve")

            @block.gpsimd
            def _(e):
                replay(e, "pool")

            @block.sync
            def _(e):
                replay(e, "sp")


S = 2048
D = 1024
NH = 12
DH = 128
DFF = 2816
NFC = DFF // 128
INW = 8704
K0, V0, GLUV0, GLUG0, GA0, GC0 = 1536, 3072, 4608, 5632, 6656, 7680
DIL = (1, 4, 16)
NBLK = (16, 4, 1)
RMS_EPS = 1e-6
LN_EPS = 1e-5
MASKVAL = -30000.0

C_BG, C_DWW, C_DWB, C_LNG, C_LNB, C_FFW, C_FFB = 0, 16, 264, 272, 280, 288, 420
C_VEC = 464
C_GPRE_MIX = 464
C_GPRE_FFN = C_GPRE_MIX + 1024
C_BM = C_GPRE_FFN + 1024
C_GPOST_MIX = C_BM + 12 * 256
C_GPOST_FFN = C_GPOST_MIX + 1024
NCONST = C_GPOST_FFN + 1024

KB = 1024
BASE = 17 * KB


def bcast(ap, pos, n):
    l = [list(x) for x in ap.ap]
    l.insert(1 + pos, [0, n])
    return bass.AP(ap.tensor, ap.offset, l)


def build_program(stop=None, dbg=()):
    nc = bass.Bass("TRN2", target_bir_lowering=False)
    dram_in = lambda n, s: nc.dram_tensor(n, s, F32, kind="ExternalInput").ap()
    x = dram_in("x", [S, D])
    w_in = dram_in("w_in", [D, INW])
    w_ao = dram_in("w_ao", [512, D])
    w_pw = dram_in("w_pw", [D, D])
    w_out = dram_in("w_out", [D, D])
    w_up = dram_in("w_up", [D, 2 * DFF])
    w_down = dram_in("w_down", [DFF, D])
    consts = dram_in("consts", [128, NCONST])
    out = nc.dram_tensor("out", [S, D], F32, kind="ExternalOutput").ap()
    dbg_out = {}

    w_in_v = w_in.rearrange("(kc p) n -> p kc n", p=128)
    w_ao_v = w_ao.rearrange("(kc p) n -> p kc n", p=128)
    w_pw_v = w_pw.rearrange("(kc p) n -> p kc n", p=128)
    w_out_v = w_out.rearrange("(kc p) n -> p kc n", p=128)
    w_up_v = w_up.rearrange("(kc p) n -> p kc n", p=128)
    w_down_v = w_down.rearrange("(kc p) n -> p kc n", p=128)

    with contextlib.ExitStack() as st:
        P = Prog(nc, st)
        _n = [0]

        def sbt(off_kb, shape, dt):
            _n[0] += 1
            return nc.alloc_sbuf_tensor_at("t%d" % _n[0], shape, dt, offset=int(BASE + off_kb * KB))

        PS = [st.enter_context(nc.psum_tensor("ps%d" % i, [128, 1024], F32)) for i in range(4)]
        t_bank = [TT("bank%d" % i) for i in range(8)]
        bank = lambda b: PS[b // 2][:, (b % 2) * 512:(b % 2) * 512 + 512]
        bank_bf = lambda b: PS[b // 2][:, (b % 2) * 512:(b % 2) * 512 + 512].bitcast(BF16)

        cvec = sbt(0, [128, C_VEC], F32)
        gpre_mix = sbt(2, [128, 8, 128], F32)
        gpre_ffn = sbt(6, [128, 8, 128], F32)
        ident = sbt(10, [128, 128], BF16)
        ones_bf = sbt(10.25, [128, 128], BF16)
        ones_f = sbt(10.5, [128, 128], F32)
        stat = sbt(11, [128, 256], F32)
        ring = [sbt(12 + 8 * i, [128, 4096], BF16) for i in range(3)]
        t_ring = [TT("ring%d" % i) for i in range(3)]
        t_const = TT("const")
        t_ident = TT("ident")
        t_ones = TT("ones")

        def act(out_, in_, func, reads, writes, bias=None, scale=None, accum=None):
            kw = {}
            if bias is not None:
                kw["bias"] = bias
            if scale is not None:
                kw["scale"] = scale
            if accum is not None:
                kw["accum_out"] = accum
            P.op("act", lambda e: e.activation(out=out_, in_=in_, func=func, **kw), reads=reads, writes=writes)

        def tt_op(eng, out_, in0, in1, op, reads, writes):
            P.op(eng, lambda e: e.tensor_tensor(out=out_, in0=in0, in1=in1, op=op), reads=reads, writes=writes)

        def ts_op(eng, out_, in0, s1, s2, op0, op1, reads, writes):
            if s2 is None:
                P.op(eng, lambda e: e.tensor_scalar(out_, in0, s1, None, op0), reads=reads, writes=writes)
            else:
                P.op(eng, lambda e: e.tensor_scalar(out_, in0, s1, s2, op0, op1), reads=reads, writes=writes)

        def stt_op(eng, out_, in0, scalar, in1, op0, op1, reads, writes):
            P.op(eng, lambda e: e.scalar_tensor_tensor(out=out_, in0=in0, scalar=scalar, in1=in1, op0=op0, op1=op1),
                 reads=reads, writes=writes)

        def copy_op(eng, out_, in_, reads, writes):
            P.op(eng, lambda e: e.tensor_copy(out_, in_), reads=reads, writes=writes)

        def mm(groups, reads, writes, first=True, last=True):
            def fn(e):
                ins = None
                for (o, pairs) in groups:
                    n = len(pairs)
                    for i, (l, r) in enumerate(pairs):
                        ins = e.matmul(o, lhsT=l, rhs=r, start=(first and i == 0), stop=(last and i == n - 1))
                return ins
            P.op("pe", fn, reads=reads, writes=writes)

        def dma(queue, out_, in_, semkey, reads=(), writes=()):
            return P.dma(queue, lambda e: e.dma_start(out=out_, in_=in_), semkey, reads=reads, writes=writes)

        pieces = []

        def piece_cols(src_v, col0, width, kcs, off):
            return (lambda slot, off=off, kcs=kcs, width=width:
                    slot[:, off:off + kcs * width].rearrange("p (k n) -> p k n", k=kcs),
                    src_v[:, :, col0:col0 + width])

        for g in range(3):
            pieces.append([piece_cols(w_in_v, V0 + g * 512, 512, 8, 0)])
        for js in range(4):
            for g in range(3):
                h = 4 * g + js
                pieces.append([piece_cols(w_in_v, h * 128, 128, 8, 0),
                               piece_cols(w_in_v, K0 + h * 128, 128, 8, 1024)])
        for c in range(8):
            pieces.append([piece_cols(w_in_v, GLUV0 + c * 128, 128, 8, 0),
                           piece_cols(w_in_v, GLUG0 + c * 128, 128, 8, 1024)])
        for j in range(8):
            pieces.append([piece_cols(w_ao_v, j * 128, 128, 4, 0),
                           piece_cols(w_pw_v, j * 128, 128, 8, 512),
                           piece_cols(w_in_v, GA0 + j * 128, 128, 8, 1536),
                           piece_cols(w_in_v, GC0 + j * 128, 128, 8, 2560)])
        for m in range(NFC):
            pieces.append([piece_cols(w_up_v, m * 128, 128, 8, 0),
                           piece_cols(w_up_v, DFF + m * 128, 128, 8, 1024)])
        ring_state = {"issued": 0, "next": 0}

        def ring_issue_upto(k):
            while ring_state["issued"] <= k and ring_state["issued"] < len(pieces):
                i = ring_state["issued"]
                s = i % 3
                for (dst_fn, src) in pieces[i]:
                    dma("pool", dst_fn(ring[s]), src, "d_r%d" % s, writes=[t_ring[s]])
                ring_state["issued"] += 1

        def ring_next(ahead=2):
            k = ring_state["next"]
            ring_state["next"] += 1
            ring_issue_upto(k + ahead)
            return ring[k % 3], t_ring[k % 3]

        def ring_view(slot, off, kcs, width):
            return slot[:, off:off + kcs * width].rearrange("p (k n) -> p k n", k=kcs)

        def dump(name, sb_ap, shape, dt, treads):
            if name not in dbg:
                return
            d = nc.dram_tensor("dbg_" + name, shape, dt, kind="ExternalOutput").ap()
            dbg_out[name] = TT()
            dma("sp", d, sb_ap, "d_dbg", reads=treads, writes=[dbg_out[name]])

        def finish():
            fin = list(dbg_out.values()) + final_tts
            if fin:
                P.wait_all("sp", fin)
            P.emit()

        final_tts = []

        identf = sbt(178, [128, 128], F32)
        t_identf = TT()
        xs_all = sbt(84, [128, 16, D], F32)
        t_xs = [TT() for _ in range(16)]
        for i in range(2):
            dma("sp", xs_all[:, i, :], x[i * 128:(i + 1) * 128, :], "d_x%d" % i, writes=[t_xs[i]])
        t_gpre = TT()
        dma("sp", gpre_mix[:, :, :].rearrange("p a b -> p (a b)"), consts[:, C_GPRE_MIX:C_GPRE_MIX + 1024], "d_gpre", writes=[t_gpre])
        P.op("pool", lambda e: e.memset(identf[:, :], 0.0), writes=[t_identf])
        P.op("pool", lambda e: e.affine_select(out=identf[:, :], in_=identf[:, :], compare_op=ALU.not_equal, fill=1.0,
                                               base=0, pattern=[[-1, 128]], channel_multiplier=1),
             reads=[t_identf], writes=[t_identf])
        copy_op("dve", ident[:, :], identf[:, :], [t_identf], [t_ident])
        P.op("pool", lambda e: e.memset(ones_bf[:, :], 1.0), writes=[t_ones])
        P.op("pool", lambda e: e.memset(ones_f[:, :], 1.0), writes=[t_ones])
        P.op("pool", lambda e: e.memset(stat[:, 255:256], RMS_EPS), writes=[t_ones])
        P.op("pool", lambda e: e.memset(stat[:, 254:255], LN_EPS), writes=[t_ones])

        hT = sbt(36, [128, 8, S], BF16)
        t_hT = [TT("hT%d" % i) for i in range(16)]
        xn = [sbt(172 + 2 * i, [128, D], BF16) for i in range(2)]
        t_xn = [TT(), TT()]
        junk = sbt(176, [128, D], BF16)
        t_junk = TT()
        for i in range(2, 16):
            dma("sp", xs_all[:, i, :], x[i * 128:(i + 1) * 128, :], "d_x%d" % i,
                reads=([t_xs[i - 3]] if i >= 3 else []), writes=[t_xs[i]])
        dma("sp", cvec[:, :], consts[:, 0:C_VEC], "d_const", writes=[t_const])
        dma("sp", gpre_ffn[:, :, :].rearrange("p a b -> p (a b)"), consts[:, C_GPRE_FFN:C_GPRE_FFN + 1024], "d_const", writes=[t_const])
        P.wait_all("pool", [t_xs[9]])
        ring_issue_upto(1)
        t_pair = [TT() for _ in range(8)]

        def p1_stats(pr):
            ti = t_pair[pr]
            i0_ = 2 * pr
            for i in (i0_, i0_ + 1):
                act(junk[:, :], xs_all[:, i, :], AF.Square, [t_xs[i]], [t_junk, ti], accum=stat[:, i:i + 1])
            act(stat[:, 192 + i0_:194 + i0_], stat[:, i0_:i0_ + 2], AF.Ln, [ti, t_ones], [ti], bias=stat[:, 255:256], scale=1.0 / D)
            act(stat[:, 16 + i0_:18 + i0_], stat[:, 192 + i0_:194 + i0_], AF.Exp, [ti], [ti], scale=-0.5)

        def p1_scale(i):
            ti = t_pair[i // 2]
            s2 = i % 2
            if i % 4 == 3:
                act(xn[s2][:, :], xs_all[:, i, :], AF.Copy, [t_xs[i], ti], [t_xn[s2]], scale=stat[:, 16 + i:17 + i])
            else:
                ts_op("dve", xn[s2][:, :], xs_all[:, i, :], stat[:, 16 + i:17 + i], None, ALU.mult, None, [t_xs[i], ti], [t_xn[s2]])

        def p1_tr(i):
            s2 = i % 2
            b = i % 2
            tp = bank_bf(b)

            def tr_fn(e, s2=s2, tp=tp):
                ins = None
                for c in range(8):
                    ins = e.transpose(tp[:, c * 128:(c + 1) * 128], xn[s2][:, c * 128:(c + 1) * 128], ident[:, :])
                return ins
            P.op("pe", tr_fn, reads=[t_xn[s2], t_ident], writes=[t_bank[b]])
            tt_op("dve", hT[:, :, i * 128:(i + 1) * 128], tp[:, 0:1024].rearrange("p (c t) -> p c t", c=8),
                  gpre_mix[:, :, :], ALU.mult, [t_bank[b], t_gpre], [t_hT[i]])

        p1_stats(0)
        for step in range(17):
            if step < 16:
                if step % 2 == 0 and step // 2 + 1 < 8:
                    p1_stats(step // 2 + 1)
                p1_scale(step)
            if step >= 1:
                p1_tr(step - 1)
        dump("hT", hT[:, :, :].rearrange("p a b -> p (a b)"), [128, 8 * S], BF16, t_hT)
        if stop == 1:
            finish()
            return nc
        P.barrier(engs=("act", "dve", "pool", "sp"))

        Vall = sbt(84, [128, 3, 16, 512], BF16)
        t_V = [[TT() for _ in range(16)] for _ in range(3)]
        bm = sbt(132, [128, 12, 256], F32)
        t_bm = TT()
        dma("sp", bm[:, :, :].rearrange("p a b -> p (a b)"), consts[:, C_BM:C_BM + 3072], "d_const2", writes=[t_bm])

        def tok_tile(g, j):
            if g == 0:
                return slice(j * 128, (j + 1) * 128)
            if g == 1:
                r, b = j // 4, j % 4
                return slice(512 * b + r, 512 * (b + 1), 4)
            return slice(j, S, 16)

        nev = 0
        for g in range(3):
            slot, t_slot = ring_next()
            wv = ring_view(slot, 0, 8, 512)
            for j in range(16):
                b = 2 + (nev % 2)
                sl = tok_tile(g, j)
                mm([(bank(b), [(hT[:, c, sl], wv[:, c, :]) for c in range(8)])],
                   reads=t_hT + [t_slot], writes=[t_bank[b]])
                if nev % 2 == 0:
                    act(Vall[:, g, j, :], bank(b), AF.Copy, [t_bank[b]], [t_V[g][j]])
                else:
                    copy_op("dve", Vall[:, g, j, :], bank(b), [t_bank[b]], [t_V[g][j]])
                nev += 1
        dump("Vall", Vall[:, :, :, :].rearrange("p a b c -> p (a b c)"), [128, 3 * 16 * 512], BF16, [t for l in t_V for t in l])
        if stop == 2:
            finish()
            return nc

        aT = sbt(68, [128, 4, S], BF16)
        t_aT = [TT() for _ in range(4)]
        qT = [sbt(144 + 4 * i, [128, S], BF16) for i in range(2)]
        kT = [sbt(152 + 4 * i, [128, S], BF16) for i in range(2)]
        t_q = [TT(), TT()]
        t_k = [TT(), TT()]
        accO = sbt(160, [128, S], F32)
        accD = sbt(168, [128, S], F32)
        t_accO = TT()
        t_accD = TT()
        PT = [sbt(176 + 8 * i, [128, 16, 256], BF16) for i in range(2)]
        t_PT = [[TT() for _ in range(16)] for _ in range(2)]
        ssb = [sbt(192 + i, [128, 256], F32) for i in range(4)]
        t_ssb = [TT() for _ in range(4)]

        def acc_view(acc, g, bq):
            if g == 0:
                return acc[:, 512 * bq:512 * (bq + 1)]
            if g == 1:
                return acc[:, bq::4]
            return acc[:, :].rearrange("p (i r) -> p r i", r=16)[:, 4 * bq:4 * bq + 4, :]

        def ps_view(b, g):
            if g == 2:
                return bank(b).rearrange("p (r i) -> p r i", r=4)
            return bank(b)

        scale_q = 1.0 / math.sqrt(DH)
        heads = [(js, g) for js in range(4) for g in range(3)]
        t_sc = [TT() for _ in range(4)]
        nsb_box = [0]

        def proj_units(idx):
            js, g = heads[idx]
            h = 4 * g + js
            hs = idx % 2
            slot, t_slot = ring_next()
            wq = ring_view(slot, 0, 8, 128)
            wk = ring_view(slot, 1024, 8, 128)
            units = []
            for which in range(2):
                for tb in range(4):
                    def unit(which=which, tb=tb):
                        wsel = wq if which == 0 else wk
                        dst = qT[hs] if which == 0 else kT[hs]
                        b = (which * 4 + tb) % 2
                        mm([(bank(b), [(wsel[:, c, :], hT[:, c, 512 * tb:512 * (tb + 1)]) for c in range(8)])],
                           reads=t_hT[4 * tb:4 * tb + 4] + [t_slot], writes=[t_bank[b]])
                        if g == 0:
                            o_ap, i_ap = dst[:, 512 * tb:512 * (tb + 1)], bank(b)
                        else:
                            r = DIL[g]
                            w_ = 512 // r
                            o_ap = dst[:, :].rearrange("p (r n) -> p r n", r=r)[:, :, w_ * tb:w_ * (tb + 1)]
                            i_ap = bank(b).rearrange("p (j r) -> p r j", r=r)
                        if which == 0:
                            act(o_ap, i_ap, AF.Copy, [t_bank[b]], [t_q[hs]], scale=scale_q)
                        else:
                            act(o_ap, i_ap, AF.Copy, [t_bank[b]], [t_k[hs]])
                    units.append(unit)
            return units

        def score_tile(idx, kt):
            js, g = heads[idx]
            h = 4 * g + js
            hs = idx % 2
            nb = NBLK[g]
            bb = kt % nb
            ncols = 256 if bb + 1 < nb else 128
            sl = kt % 4
            b = 2 + sl // 2
            c0 = 256 * (sl % 2)
            tsl = t_sc[sl]
            mm([(bank(b)[:, c0:c0 + ncols], [(kT[hs][:, kt * 128:(kt + 1) * 128], qT[hs][:, kt * 128:kt * 128 + ncols])])],
               reads=[t_q[hs], t_k[hs]], writes=[tsl] + ([t_bank[b]] if (idx == 0 and kt < 4) else []))
            sb_i = nsb_box[0] % 4
            nsb_box[0] += 1
            tt_op("dve", ssb[sb_i][:, 0:ncols], bank(b)[:, c0:c0 + ncols], bm[:, h, 0:ncols], ALU.add,
                  [tsl, t_bm], [t_ssb[sb_i]])
            act(PT[hs][:, kt, 0:ncols], ssb[sb_i][:, 0:ncols], AF.Exp, [t_ssb[sb_i]], [t_PT[hs][kt]])

        def pv_head(idx):
            js, g = heads[idx]
            hs = idx % 2
            nb = NBLK[g]
            for bq in range(4):
                bo = 4 + 2 * (bq % 2)
                bd = bo + 1
                groups = []
                rd = [t_ones]
                for qi in range(4):
                    qb = 4 * bq + qi
                    pcs = [(qb, slice(0, 128))]
                    if qb % nb > 0:
                        pcs.append((qb - 1, slice(128, 256)))
                    groups.append((bank(bo)[:, qi * 128:(qi + 1) * 128],
                                   [(Vall[:, g, kt, js * 128:(js + 1) * 128], PT[hs][:, kt, cs]) for (kt, cs) in pcs]))
                    for (kt, cs) in pcs:
                        rd.append(t_PT[hs][kt])
                        rd.append(t_V[g][kt])
                q0 = 4 * bq
                dpairs = [(ones_bf[:, :], PT[hs][:, q0:q0 + 4, 0:128])]
                if nb > 1:
                    lo = 1 if (q0 % nb == 0) else 0
                    dpairs.append((ones_bf[:, :], PT[hs][:, q0 + lo - 1:q0 + 3, 128:256]))
                    if lo == 0 or q0 > 0:
                        rd.append(t_PT[hs][max(q0 - 1, 0)])

                def pv_fn(e, groups=groups, dpairs=dpairs, bd=bd, lo=(1 if (nb > 1 and q0 % nb == 0) else 0)):
                    ins = None
                    for (o, pairs) in groups:
                        n = len(pairs)
                        for i, (l, r) in enumerate(pairs):
                            ins = e.matmul(o, lhsT=l, rhs=r, start=(i == 0), stop=(i == n - 1))
                    ins = e.matmul(bank(bd)[:, 0:512], lhsT=dpairs[0][0], rhs=dpairs[0][1], start=True, stop=(len(dpairs) == 1))
                    if len(dpairs) > 1:
                        ins = e.matmul(bank(bd)[:, 128 * lo:512], lhsT=dpairs[1][0], rhs=dpairs[1][1], start=False, stop=True,
                                       skip_group_check=True)
                    return ins
                P.op("pe", pv_fn, reads=rd, writes=[t_bank[bo], t_bank[bd]])
                vo = acc_view(accO, g, bq)
                vd = acc_view(accD, g, bq)
                if g == 0:
                    act(vo, ps_view(bo, g), AF.Copy, [t_bank[bo]], [t_accO])
                    copy_op("dve", vd, ps_view(bd, g), [t_bank[bd]], [t_accD])
                else:
                    tt_op("dve", vo, ps_view(bo, g), vo, ALU.add, [t_bank[bo], t_accO], [t_accO])
                    tt_op("dve", vd, ps_view(bd, g), vd, ALU.add, [t_bank[bd], t_accD], [t_accD])

        def norm_piece(js, q):
            blk = slice(512 * q, 512 * (q + 1))
            act(accD[:, blk], accD[:, blk], AF.Ln, [t_accD], [t_accD])
            act(accD[:, blk], accD[:, blk], AF.Exp, [t_accD], [t_accD], scale=-1.0)
            tt_op("dve", aT[:, js, blk], accO[:, blk], accD[:, blk], ALU.mult, [t_accO, t_accD], [t_aT[js]])

        diag0 = sbt(196, [128, 31, 128], BF16)
        t_diag0 = TT()
        P.op("dve", lambda e: e.tensor_tensor(out=diag0[:, :, :], in0=bcast(ident[:, :], 0, 31),
                                              in1=bcast(cvec[:, C_DWW:C_DWW + 31], 1, 128), op=ALU.mult),
             reads=[t_ident, t_const], writes=[t_diag0])
        for u_ in proj_units(0):
            u_()
        pending = []
        for idx in range(len(heads)):
            nxt = proj_units(idx + 1) if idx + 1 < len(heads) else []
            for kt in range(16):
                score_tile(idx, kt)
                if kt % 2 == 1 and nxt:
                    nxt.pop(0)()
                if kt % 2 == 1 and pending:
                    norm_piece(*pending.pop(0))
            while nxt:
                nxt.pop(0)()
            pv_head(idx)
            if heads[idx][1] == 2:
                pending = [(heads[idx][0], q) for q in range(4)]
        for pc in pending:
            norm_piece(*pc)
        dump("aT", aT[:, :, :].rearrange("p a b -> p (a b)"), [128, 4 * S], BF16, t_aT)
        if stop == 3:
            finish()
            return nc
        ring_issue_upto(ring_state["next"] + 1)
        P.barrier(engs=("act", "dve", "pool", "sp"))

        csT = sbt(84, [128, 8, S], BF16)
        t_csT = [[TT() for _ in range(4)] for _ in range(8)]
        cc = sbt(116, [128, 8, S], BF16)
        t_cc = [[TT() for _ in range(4)] for _ in range(8)]
        acc1 = sbt(148, [128, S], F32)
        acc2 = sbt(156, [128, S], F32)
        t_acc1 = [TT() for _ in range(4)]
        t_acc2 = [TT() for _ in range(4)]
        cg = [sbt(164 + 4.25 * i, [128, 2176], BF16) for i in range(2)]
        t_cg = [[TT() for _ in range(5)] for _ in range(2)]
        diag = [diag0, sbt(172.5, [128, 31, 128], BF16)]
        t_diag = [t_diag0, TT()]
        sg = [sbt(188 + 2 * i, [128, 512], F32) for i in range(2)]
        t_sg = [TT(), TT()]
        sq = [sbt(192 + 2 * i, [128, 512], F32) for i in range(2)]
        t_sq = [TT(), TT()]
        for i in range(2):
            P.op("pool", (lambda i: lambda e: e.memset(cg[i][:, 0:30], 0.0))(i), writes=[t_cg[i][0]])
        nsg = 0
        nsq = 0
        for c in range(8):
            cb = c % 2
            slot, t_slot = ring_next()
            wval = ring_view(slot, 0, 8, 128)
            wgate = ring_view(slot, 1024, 8, 128)
            def build_diag(cn):
                P.op("dve", (lambda cb, c: lambda e: e.tensor_tensor(
                    out=diag[cb][:, :, :], in0=bcast(ident[:, :], 0, 31),
                    in1=bcast(cvec[:, C_DWW + c * 31:C_DWW + (c + 1) * 31], 1, 128), op=ALU.mult))(cn % 2, cn),
                    reads=[t_ident, t_const], writes=[t_diag[cn % 2]])
            for tb in range(4):
                bv, bg = 0 + 2 * (tb % 2), 1 + 2 * (tb % 2)
                mm([(bank(bv), [(wval[:, k, :], hT[:, k, 512 * tb:512 * (tb + 1)]) for k in range(8)]),
                    (bank(bg), [(wgate[:, k, :], hT[:, k, 512 * tb:512 * (tb + 1)]) for k in range(8)])],
                   reads=t_hT[4 * tb:4 * tb + 4] + [t_slot], writes=[t_bank[bv], t_bank[bg]])
                si = nsg % 2
                nsg += 1
                act(sg[si][:, :], bank(bg), AF.Sigmoid, [t_bank[bg]], [t_sg[si]])
                tt_op("dve", cg[cb][:, 30 + 512 * tb:30 + 512 * (tb + 1)], bank(bv), sg[si][:, :], ALU.mult,
                      [t_bank[bv], t_sg[si]], [t_cg[cb][1 + tb]])
            if c + 1 < 8:
                build_diag(c + 1)
            for tb in range(4):
                bc = 4 + (tb % 2)
                mm([(bank(bc), [(diag[cb][:, k, :], cg[cb][:, 512 * tb + k:512 * tb + k + 512]) for k in range(31)])],
                   reads=[t_diag[cb], t_cg[cb][tb], t_cg[cb][1 + tb]], writes=[t_bank[bc]])
                qi = nsq % 2
                nsq += 1
                blk = slice(512 * tb, 512 * (tb + 1))
                act(sg[qi][:, :], bank(bc), AF.Identity, [t_bank[bc]], [t_sg[qi]], bias=cvec[:, C_DWB + c:C_DWB + c + 1])
                copy_op("pool", cc[:, c, blk], sg[qi][:, :], [t_sg[qi]], [t_cc[c][tb]])
                act(sq[qi][:, :], sg[qi][:, :], AF.Square, [t_sg[qi]], [t_sq[qi]])
                if c == 0:
                    copy_op("dve", acc1[:, blk], sg[qi][:, :], [t_sg[qi]], [t_acc1[tb]])
                    copy_op("pool", acc2[:, blk], sq[qi][:, :], [t_sq[qi]], [t_acc2[tb]])
                else:
                    tt_op("dve", acc1[:, blk], sg[qi][:, :], acc1[:, blk], ALU.add, [t_sg[qi], t_acc1[tb]], [t_acc1[tb]])
                    tt_op("pool", acc2[:, blk], sq[qi][:, :], acc2[:, blk], ALU.add, [t_sq[qi], t_acc2[tb]], [t_acc2[tb]])
        dump("cc", cc[:, :, :].rearrange("p a b -> p (a b)"), [128, 8 * S], BF16, [t for l in t_cc for t in l])
        def ln_stats(tb):
            blk = slice(512 * tb, 512 * (tb + 1))
            b1, b2 = 0 + 2 * (tb % 2), 1 + 2 * (tb % 2)
            mm([(bank(b1), [(ones_f[:, :], acc1[:, blk])]), (bank(b2), [(ones_f[:, :], acc2[:, blk])])],
               reads=[t_ones, t_acc1[tb], t_acc2[tb]], writes=[t_bank[b1], t_bank[b2]])
            act(acc1[:, blk], bank(b1), AF.Copy, [t_bank[b1]], [t_acc1[tb]], scale=1.0 / D)
            qi = tb % 2
            act(sq[qi][:, :], acc1[:, blk], AF.Square, [t_acc1[tb]], [t_sq[qi]])
            stt_op("dve", acc2[:, blk], bank(b2), 1.0 / D, sq[qi][:, :], ALU.mult, ALU.subtract, [t_bank[b2], t_sq[qi]], [t_acc2[tb]])
            act(acc2[:, blk], acc2[:, blk], AF.Ln, [t_acc2[tb], t_ones], [t_acc2[tb]], bias=stat[:, 254:255])
            act(acc2[:, blk], acc2[:, blk], AF.Exp, [t_acc2[tb]], [t_acc2[tb]], scale=-0.5)
            act(negm[:, blk], acc1[:, blk], AF.Copy, [t_acc1[tb]], [t_negm[tb]], scale=-1.0)
        nz_box = [0]
        negm = sbt(180.5, [128, S], BF16)
        t_negm = [TT() for _ in range(4)]
        ztmp = [sg[0], sg[1], sq[0], sq[1]]
        t_ztmp = [t_sg[0], t_sg[1], t_sq[0], t_sq[1]]

        def ln_apply(tb):
            blk = slice(512 * tb, 512 * (tb + 1))
            for c in range(8):
                zi = nz_box[0] % 4
                bl = 4 + (nz_box[0] % 4)
                nz_box[0] += 1
                mm([(bank(bl), [(ident[:, :], cc[:, c, blk]), (ident[:, :], negm[:, blk])])],
                   reads=[t_ident, t_cc[c][tb], t_negm[tb]], writes=[t_bank[bl]])
                tt_op("dve", ztmp[zi][:, :], bank(bl), acc2[:, blk], ALU.mult, [t_bank[bl], t_acc2[tb]], [t_ztmp[zi]])
                act(csT[:, c, blk], ztmp[zi][:, :], AF.Silu, [t_ztmp[zi], t_const], [t_csT[c][tb]],
                    bias=cvec[:, C_LNB + c:C_LNB + c + 1], scale=cvec[:, C_LNG + c:C_LNG + c + 1])

        if stop == 4:
            for tb in range(4):
                ln_stats(tb)
            for tb in range(4):
                ln_apply(tb)
            dump("csT", csT[:, :, :].rearrange("p a b -> p (a b)"), [128, 8 * S], BF16, [t for l in t_csT for t in l])
            finish()
            return nc
        ring_issue_upto(ring_state["next"] + 1)

        mixT = sbt(116, [128, 8, S], BF16)
        t_mix = t_cc
        sga = [sbt(164 + 2 * i, [128, 512], F32) for i in range(2)]
        sgc = [sbt(168 + 2 * i, [128, 512], F32) for i in range(2)]
        m1 = [sbt(172 + 2 * i, [128, 512], F32) for i in range(2)]
        m2 = [sbt(176 + 2 * i, [128, 512], F32) for i in range(2)]
        t_sga, t_sgc, t_m1, t_m2 = [TT(), TT()], [TT(), TT()], [TT(), TT()], [TT(), TT()]
        Wout = sbt(180, [128, 8, D], BF16)
        t_Wout = TT()
        it_box = [0]
        ln_window = [True]
        slots5 = {}

        def blk5(j, tb):
            if j not in slots5:
                if j == 2:
                    for nbk in range(2):
                        dma("pool", Wout[:, :, 512 * nbk:512 * (nbk + 1)], w_out_v[:, :, 512 * nbk:512 * (nbk + 1)], "d_wout",
                            writes=[t_Wout, t_sg[0], t_sg[1], t_sq[0], t_sq[1], t_diag[1]] + t_negm)
                slots5[j] = ring_next(ahead=(1 if j == 1 else 2))
            slot, t_slot = slots5[j]
            wao = ring_view(slot, 0, 4, 128)
            wpw = ring_view(slot, 512, 8, 128)
            wga = ring_view(slot, 1536, 8, 128)
            wgc = ring_view(slot, 2560, 8, 128)
            blk = slice(512 * tb, 512 * (tb + 1))
            p = 0 if ln_window[0] else it_box[0] % 2
            it_box[0] += 1
            bya, byc, bga, bgc = 4 * p, 4 * p + 1, 4 * p + 2, 4 * p + 3
            mm([(bank(bga), [(wga[:, k, :], hT[:, k, blk]) for k in range(8)]),
                (bank(bgc), [(wgc[:, k, :], hT[:, k, blk]) for k in range(8)]),
                (bank(bya), [(wao[:, k, :], aT[:, k, blk]) for k in range(4)])],
               reads=t_hT[4 * tb:4 * tb + 4] + t_aT + [t_slot],
               writes=[t_bank[bya], t_bank[bga], t_bank[bgc]])
            mm([(bank(byc), [(wpw[:, k, :], csT[:, k, blk]) for k in range(8)])],
               reads=[t_csT[k][tb] for k in range(8)] + [t_slot], writes=[t_bank[byc]])
            act(sga[p][:, :], bank(bga), AF.Sigmoid, [t_bank[bga], t_const], [t_sga[p]], bias=cvec[:, C_BG + j:C_BG + j + 1])
            act(sgc[p][:, :], bank(bgc), AF.Sigmoid, [t_bank[bgc], t_const], [t_sgc[p]], bias=cvec[:, C_BG + 8 + j:C_BG + 9 + j])
            tt_op("dve", m1[p][:, :], bank(bya), sga[p][:, :], ALU.mult, [t_bank[bya], t_sga[p]], [t_m1[p]])
            tt_op("dve", m2[p][:, :], bank(byc), sgc[p][:, :], ALU.mult, [t_bank[byc], t_sgc[p]], [t_m2[p]])
            tt_op("dve", mixT[:, j, blk], m1[p][:, :], m2[p][:, :], ALU.add, [t_m1[p], t_m2[p]], [t_mix[j][tb]])

        seq5 = [("st", 0), ("st", 1), ("st", 2), ("st", 3), ("ln", 0), ("ln", 1), ("b", 0, 0), ("ln", 2), ("b", 0, 1), ("b", 1, 0),
                ("ln", 3), ("b", 0, 2), ("b", 1, 1), ("b", 0, 3), ("b", 1, 2), ("b", 1, 3)]
        seq5 += [("b", j, tb) for j in range(2, 8) for tb in range(4)]
        for item in seq5:
            if item[0] == "st":
                ln_stats(item[1])
            elif item[0] == "ln":
                ln_apply(item[1])
                if item[1] == 3:
                    ln_window[0] = False
            else:
                blk5(item[1], item[2])
        dump("mixT", mixT[:, :, :].rearrange("p a b -> p (a b)"), [128, 8 * S], BF16, [t for l in t_mix for t in l])
        if stop == 5:
            finish()
            return nc
        ring_issue_upto(ring_state["next"] + 1)
        P.barrier(engs=("act", "dve", "pool", "sp"))

        h2T = sbt(36, [128, 8, S], BF16)
        t_h2T = [TT() for _ in range(16)]
        NS6 = 4
        xs = [sbt(84 + 4 * i, [128, D], F32) for i in range(NS6)]
        t_xs2 = [TT() for _ in range(NS6)]
        x1s = [sbt(100 + 4 * i, [128, D], F32) for i in range(NS6)]
        t_x1s = [TT() for _ in range(NS6)]
        xn2 = [sbt(196 + 2 * i, [128, D], BF16) for i in range(2)]
        t_xn2 = [TT(), TT()]
        junk2 = sbt(200, [128, D], BF16)
        t_junk2 = TT()
        gpost = sbt(148, [128, D], F32)
        t_gpost = TT()
        WdA = sbt(156, [128, 11, D], BF16)
        t_WdA = TT()
        dma("sp", gpost[:, :], consts[:, C_GPOST_MIX:C_GPOST_MIX + 1024], "d_gp", writes=[t_gpost])
        dma("pool", WdA[:, :, :], w_down_v[:, 0:11, :], "d_wda", writes=[t_WdA])
        t_x1d = [TT() for _ in range(16)]
        t_s2 = [TT() for _ in range(16)]
        t_s3 = [TT() for _ in range(16)]
        eps_ap = stat[:, 255:256]

        def act_rstd(col_ss, col_tmp, col_out, t):
            act(col_tmp, col_ss, AF.Ln, [t, t_ones], [t], bias=eps_ap, scale=1.0 / D)
            act(col_out, col_tmp, AF.Exp, [t], [t], scale=-0.5)

        def p6_A(i):
            s3 = i % NS6
            dma("sp", xs[s3][:, :], x[i * 128:(i + 1) * 128, :], "d_xs%d" % s3, writes=[t_xs2[s3]])
            pp = i % 3
            b0, b1 = 2 * pp, 2 * pp + 1
            mm([(bank(b0), [(mixT[:, k, i * 128:(i + 1) * 128], Wout[:, k, 0:512]) for k in range(8)]),
                (bank(b1), [(mixT[:, k, i * 128:(i + 1) * 128], Wout[:, k, 512:1024]) for k in range(8)])],
               reads=[t_mix[k][i // 4] for k in range(8)] + [t_Wout], writes=[t_bank[b0], t_bank[b1]])
            act(junk2[:, :], PS[pp][:, :], AF.Square, [t_bank[b0], t_bank[b1]], [t_junk2, t_s2[i]], accum=stat[:, 32 + i:33 + i])
            act_rstd(stat[:, 32 + i:33 + i], stat[:, 128 + i:129 + i], stat[:, 48 + i:49 + i], t_s2[i])

        def p6_B1(i):
            s3 = i % NS6
            pp = i % 3
            b0, b1 = 2 * pp, 2 * pp + 1
            stt_op("dve", x1s[s3][:, :], PS[pp][:, :], stat[:, 48 + i:49 + i], gpost[:, :], ALU.mult, ALU.mult,
                   [t_bank[b0], t_bank[b1], t_s2[i], t_gpost], [t_x1s[s3]])
            tt_op("dve", x1s[s3][:, :], x1s[s3][:, :], xs[s3][:, :], ALU.add, [t_x1s[s3], t_xs2[s3]], [t_x1s[s3]])
            dma("sp", out[i * 128:(i + 1) * 128, :], x1s[s3][:, :], "d_x1w", reads=[t_x1s[s3]], writes=[t_x1d[i]])

        def p6_B2(i):
            s3 = i % NS6
            act(junk2[:, :], x1s[s3][:, :], AF.Square, [t_x1s[s3]], [t_junk2, t_s3[i]], accum=stat[:, 64 + i:65 + i])
            act_rstd(stat[:, 64 + i:65 + i], stat[:, 144 + i:145 + i], stat[:, 80 + i:81 + i], t_s3[i])
            s2 = i % 2
            if i % 2 == 0:
                act(xn2[s2][:, :], x1s[s3][:, :], AF.Copy, [t_x1s[s3], t_s3[i]], [t_xn2[s2]], scale=stat[:, 80 + i:81 + i])
            else:
                ts_op("dve", xn2[s2][:, :], x1s[s3][:, :], stat[:, 80 + i:81 + i], None, ALU.mult, None,
                      [t_x1s[s3], t_s3[i]], [t_xn2[s2]])

        def p6_C(i):
            s2 = i % 2
            bt = 6 + (i % 2)
            tp = bank_bf(bt)

            def tr_fn2(e, s2=s2, tp=tp):
                ins = None
                for c in range(8):
                    ins = e.transpose(tp[:, c * 128:(c + 1) * 128], xn2[s2][:, c * 128:(c + 1) * 128], ident[:, :])
                return ins
            P.op("pe", tr_fn2, reads=[t_xn2[s2], t_ident], writes=[t_bank[bt]])
            tt_op("dve", h2T[:, :, i * 128:(i + 1) * 128], tp[:, 0:1024].rearrange("p (c t) -> p c t", c=8),
                  gpre_ffn[:, :, :], ALU.mult, [t_bank[bt], t_const], [t_h2T[i]])

        for it in range(16 + 3):
            if it < 16:
                p6_A(it)
            if 0 <= it - 1 < 16:
                p6_B1(it - 1)
            if 0 <= it - 2 < 16:
                p6_B2(it - 2)
            if 0 <= it - 3 < 16:
                p6_C(it - 3)
        for t in t_x1d:
            t.w = t_x1d[-1].w
        dump("h2T", h2T[:, :, :].rearrange("p a b -> p (a b)"), [128, 8 * S], BF16, t_h2T)
        if stop == 6:
            final_tts.extend(t_x1d)
            finish()
            return nc
        ring_issue_upto(ring_state["next"] + 1)
        P.barrier(engs=("act", "dve", "pool", "sp"))

        fT = sbt(68, [128, NFC, S], BF16)
        t_fT = [[TT() for _ in range(4)] for _ in range(NFC)]
        NS8 = 3
        ubw = [sbt(178 + 8.25 * w, [128, 2050], F32) for w in range(2)]
        t_ubb = [[TT() for _ in range(4)] for _ in range(2)]
        t_ubpad = [TT(), TT()]
        t0 = [[sbt(194.5 + 2 * (NS8 * w + i), [128, 512], F32) for i in range(NS8)] for w in range(2)]
        t_t0 = [[TT() for _ in range(NS8)] for _ in range(2)]
        slots8 = {}
        for w in range(2):
            P.op("pool", (lambda w: lambda e: e.memset(ubw[w][:, 0:2], 0.0))(w), writes=[t_ubpad[w]])

        def ub_prev(w, tb):
            return t_ubb[w][tb - 1] if tb > 0 else t_ubpad[w]

        def p8_s1(n):
            m, tb = n // 4, n % 4
            if tb == 0:
                slot, t_slot = ring_next()
                slots8[m] = (slot, t_slot)
            slot, t_slot = slots8[m]
            wsel = [ring_view(slot, 0, 8, 128), ring_view(slot, 1024, 8, 128)]
            blk = slice(512 * tb, 512 * (tb + 1))
            u = n % NS8
            bks = [(2 * n) % 6, (2 * n + 1) % 6]
            for w in range(2):
                mm([(bank(bks[w]), [(wsel[w][:, k, :], h2T[:, k, blk]) for k in range(8)])],
                   reads=t_h2T[4 * tb:4 * tb + 4] + [t_slot], writes=[t_bank[bks[w]]])
            for w in range(2):
                act(ubw[w][:, 2 + 512 * tb:2 + 512 * (tb + 1)], bank(bks[w]), AF.Copy, [t_bank[bks[w]]], [t_ubb[w][tb]])
            for w in range(2):
                ch = m if w == 0 else NFC + m
                fw2 = cvec[:, C_FFW + 3 * ch + 2:C_FFW + 3 * ch + 3]
                fb = cvec[:, C_FFB + ch:C_FFB + ch + 1]
                act(t0[w][u][:, :], ubw[w][:, 2 + 512 * tb:2 + 512 * (tb + 1)], AF.Identity, [t_ubb[w][tb], t_const], [t_t0[w][u]],
                    bias=fb, scale=fw2)

        def p8_s2(n):
            m, tb = n // 4, n % 4
            u = n % NS8
            for w in range(2):
                ch = m if w == 0 else NFC + m
                fw1 = cvec[:, C_FFW + 3 * ch + 1:C_FFW + 3 * ch + 2]
                stt_op("dve", t0[w][u][:, :], ubw[w][:, 1 + 512 * tb:1 + 512 * (tb + 1)], fw1, t0[w][u][:, :], ALU.mult, ALU.add,
                       [t_ubb[w][tb], ub_prev(w, tb), t_t0[w][u]], [t_t0[w][u]])
            for w in range(2):
                ch = m if w == 0 else NFC + m
                fw0 = cvec[:, C_FFW + 3 * ch + 0:C_FFW + 3 * ch + 1]
                stt_op("dve", t0[w][u][:, :], ubw[w][:, 512 * tb:512 * (tb + 1)], fw0, t0[w][u][:, :], ALU.mult, ALU.add,
                       [t_ubb[w][tb], ub_prev(w, tb), t_t0[w][u]], [t_t0[w][u]])

        def p8_s3a(n):
            u = n % NS8
            act(t0[0][u][:, :], t0[0][u][:, :], AF.Gelu_apprx_tanh, [t_t0[0][u]], [t_t0[0][u]])

        def p8_s3b(n):
            m, tb = n // 4, n % 4
            blk = slice(512 * tb, 512 * (tb + 1))
            u = n % NS8
            tt_op("dve", fT[:, m, blk], t0[0][u][:, :], t0[1][u][:, :], ALU.mult, [t_t0[0][u], t_t0[1][u]], [t_fT[m][tb]])

        NB8 = NFC * 4
        for it in range(NB8 + 2):
            if 0 <= it - 2 < NB8:
                p8_s3a(it - 2)
            if it < NB8:
                p8_s1(it)
            if 0 <= it - 1 < NB8:
                p8_s2(it - 1)
            if 0 <= it - 2 < NB8:
                p8_s3b(it - 2)
        dump("fT", fT[:, :, :].rearrange("p a b -> p (a b)"), [128, NFC * S], BF16, [t for l in t_fT for t in l])
        if stop == 8:
            final_tts.extend(t_x1d)
            finish()
            return nc

        WdB = sbt(36, [128, 11, D], BF16)
        t_WdB = TT()
        dma("pool", WdB[:, :, :], w_down_v[:, 11:22, :], "d_wdb", writes=[t_WdB] + t_h2T)
        old8 = [t for l in t_ubb for t in l] + t_ubpad + [t for l in t_t0 for t in l]
        gpost2 = sbt(194, [128, D], F32)
        t_gpost2 = TT()
        dma("sp", gpost2[:, :], consts[:, C_GPOST_FFN:C_GPOST_FFN + 1024], "d_gp", writes=[t_gpost2] + old8)
        x1r = [sbt(178 + 4 * i, [128, D], F32) for i in range(2)]
        t_x1r = [TT(), TT()]
        osl = [sbt(186 + 4 * i, [128, D], F32) for i in range(2)]
        t_osl = [TT(), TT()]
        junk3 = sbt(58, [128, D], BF16)
        t_junk3 = TT()
        t_s4 = [TT() for _ in range(16)]
        t_outd = []

        def wd(mi, cs):
            return WdA[:, mi, cs] if mi < 11 else WdB[:, mi - 11, cs]

        def pf_mm(i, part):
            pp = (i + 3) % 4
            b0, b1 = 2 * pp, 2 * pp + 1
            rng = range(NFC) if part is None else (range(0, 11) if part == 0 else range(11, NFC))
            rd = [t_fT[mi][i // 4] for mi in rng] + ([t_WdA] if part in (None, 0) else []) + ([t_WdB] if part in (None, 1) else [])
            mm([(bank(b0), [(fT[:, mi, i * 128:(i + 1) * 128], wd(mi, slice(0, 512))) for mi in rng]),
                (bank(b1), [(fT[:, mi, i * 128:(i + 1) * 128], wd(mi, slice(512, 1024))) for mi in rng])],
               reads=rd, writes=[t_bank[b0], t_bank[b1]], first=(part in (None, 0)), last=(part in (None, 1)))

        def pf_A(i):
            s2 = i % 2
            extra = (old8 if i < 2 else [])
            dma("sp", x1r[s2][:, :], out[i * 128:(i + 1) * 128, :], "d_x1r%d" % s2, reads=[t_x1d[i]], writes=[t_x1r[s2]] + extra)
            pp = (i + 3) % 4
            b0, b1 = 2 * pp, 2 * pp + 1
            act(junk3[:, :], PS[pp][:, :], AF.Square, [t_bank[b0], t_bank[b1]], [t_junk3, t_s4[i]] + (t_h2T if i == 0 else []),
                accum=stat[:, 96 + i:97 + i])
            act_rstd(stat[:, 96 + i:97 + i], stat[:, 160 + i:161 + i], stat[:, 112 + i:113 + i], t_s4[i])

        def pf_B(i):
            s2 = i % 2
            pp = (i + 3) % 4
            b0, b1 = 2 * pp, 2 * pp + 1
            extra = (old8 if i < 2 else [])
            stt_op("dve", osl[s2][:, :], PS[pp][:, :], stat[:, 112 + i:113 + i], gpost2[:, :], ALU.mult, ALU.mult,
                   [t_bank[b0], t_bank[b1], t_s4[i], t_gpost2], [t_osl[s2]] + extra)
            tt_op("dve", osl[s2][:, :], osl[s2][:, :], x1r[s2][:, :], ALU.add, [t_osl[s2], t_x1r[s2]], [t_osl[s2]])
            tt = TT()
            dma("sp", out[i * 128:(i + 1) * 128, :], osl[s2][:, :], "d_out", reads=[t_osl[s2]], writes=[tt])
            t_outd.append(tt)

        for i in range(4):
            pf_mm(i, 0)
        for i in range(4):
            pf_mm(i, 1)
            pf_A(i)
            if i >= 1:
                pf_B(i - 1)
        for it in range(4, 16 + 1):
            if it < 16:
                pf_mm(it, None)
                pf_A(it)
            if 0 <= it - 1 < 16:
                pf_B(it - 1)
        for t in t_outd:
            t.w = t_outd[-1].w
        final_tts.extend(t_outd)
        finish()
    return nc


def _t5_bucket_np(dist):
    max_exact = 16
    d = np.maximum(dist, 1).astype(np.float32)
    val = np.log(d / np.float32(max_exact)) / np.float32(math.log(2048 / max_exact)) * np.float32(32 - max_exact)
    large = max_exact + val.astype(np.int32)
    large = np.minimum(large, 31)
    return np.where(dist < max_exact, dist, large)


def _vec_layout(v, chunks):
    return np.ascontiguousarray(np.asarray(v, np.float32).reshape(chunks, 128).T)


def make_consts(inp):
    c = np.zeros((128, NCONST), np.float32)
    c[:, C_BG:C_BG + 16] = _vec_layout(inp["b_gate"][0], 16)
    dww = np.asarray(inp["conv_dw_w"][0], np.float32)
    c[:, C_DWW:C_DWW + 248] = dww.reshape(31, 8, 128).transpose(2, 1, 0).reshape(128, 248)
    c[:, C_DWB:C_DWB + 8] = _vec_layout(inp["conv_dw_b"][0], 8)
    c[:, C_LNG:C_LNG + 8] = _vec_layout(inp["conv_ln_g"][0], 8)
    c[:, C_LNB:C_LNB + 8] = _vec_layout(inp["conv_ln_b"][0], 8)
    ffw = np.asarray(inp["ffn_conv_w"][0], np.float32)
    c[:, C_FFW:C_FFW + 132] = ffw.reshape(3, 44, 128).transpose(2, 1, 0).reshape(128, 132)
    c[:, C_FFB:C_FFB + 44] = _vec_layout(inp["ffn_conv_b"][0], 44)
    gm = _vec_layout(inp["norm_mix_pre"][0], 8)
    c[:, C_GPRE_MIX:C_GPRE_MIX + 1024] = np.repeat(gm[:, :, None], 128, axis=2).reshape(128, 1024)
    gf = _vec_layout(inp["norm_ffn_pre"][0], 8)
    c[:, C_GPRE_FFN:C_GPRE_FFN + 1024] = np.repeat(gf[:, :, None], 128, axis=2).reshape(128, 1024)
    rb = np.asarray(inp["rel_bias"], np.float32)
    p = np.arange(128)[:, None]
    qq = np.arange(256)[None, :]
    rel = qq - p
    valid = (rel >= 0) & (rel <= 128)
    bmx = np.zeros((128, 12, 256), np.float32)
    for h in range(12):
        r = DIL[h // 4]
        bucket = _t5_bucket_np(np.maximum(rel, 0) * r)
        bmx[:, h, :] = np.where(valid, rb[bucket, h], np.float32(MASKVAL))
    c[:, C_BM:C_BM + 3072] = bmx.reshape(128, 3072)
    c[:, C_GPOST_MIX:C_GPOST_MIX + 1024] = np.broadcast_to(np.asarray(inp["norm_mix_post"][0], np.float32)[None, :], (128, 1024))
    c[:, C_GPOST_FFN:C_GPOST_FFN + 1024] = np.broadcast_to(np.asarray(inp["norm_ffn_post"][0], np.float32)[None, :], (128, 1024))
    return c


def make_in_maps(inp, n_cores=8):
    f = lambda a: np.ascontiguousarray(np.asarray(a, np.float32))
    consts = make_consts(inp)
    shared = {
        "w_in": f(inp["w_in"][0]), "w_ao": f(inp["w_attn_out"][0]), "w_pw": f(inp["conv_pw_w"][0]),
        "w_out": f(inp["w_out"][0]), "w_up": f(inp["w_up"][0]), "w_down": f(inp["w_down"][0]),
        "consts": consts,
    }
    xs = f(inp["x"])
    return [dict(shared, x=xs[b]) for b in range(n_cores)]


_NC_CACHE = {}


def kernel(**inputs):
    if "nc" not in _NC_CACHE:
        _NC_CACHE["nc"] = build_program()
    nc = _NC_CACHE["nc"]
    in_maps = make_in_maps(inputs)
    res = run_bass_kernel_spmd(nc, in_maps, core_ids=list(range(8)))
    return np.stack([np.asarray(r["out"], np.float32) for r in res.results], axis=0)
```
